# Optimizing a Trainium2 kernel written in Bass

```python
import math
import jax, jax.numpy as jnp
from jax import lax
import numpy as np

D_MODEL = 1024
BATCH = 8
SEQ = 4096
DEPTH = 1

PLE_DIM = 256
D_RG = D_MODEL // 2
RG_BLOCKS = 8
RG_BLOCK = D_RG // RG_BLOCKS
CONV_WIDTH = 4
RG_C = 8.0
D_HG = D_MODEL // 2
HG_HEAD_DIM = 128
HG_HEADS = D_HG // HG_HEAD_DIM
HG_CHUNK = 64
D_MIX = D_RG + D_HG
D_IN = 2 * D_RG + 4 * D_HG
EPS = 1e-6

kernel_name = "hymba_style_rglru_hgrn2_block"


def rms_norm(x, w):
    xf = x.astype(jnp.float32)
    y = xf * lax.rsqrt(jnp.mean(xf * xf, axis=-1, keepdims=True) + EPS)
    return (y * w.astype(jnp.float32)).astype(x.dtype)


def causal_depthwise_conv(x, w, b):
    T = x.shape[1]
    xp = jnp.pad(x, ((0, 0), (CONV_WIDTH - 1, 0), (0, 0)))
    y = b
    for j in range(CONV_WIDTH):
        y = y + xp[:, j:j + T] * w[j]
    return y


def rg_lru(x, wa, ba, wx, bx, lam):
    B, T, _ = x.shape
    xf = x.astype(jnp.float32)
    xb = xf.reshape(B, T, RG_BLOCKS, RG_BLOCK)
    r = jax.nn.sigmoid(jnp.einsum('btgi,gij->btgj', xb, wa.astype(jnp.float32)).reshape(B, T, D_RG) + ba)
    i = jax.nn.sigmoid(jnp.einsum('btgi,gij->btgj', xb, wx.astype(jnp.float32)).reshape(B, T, D_RG) + bx)
    log_a = -RG_C * r * jax.nn.softplus(-lam.astype(jnp.float32))
    a = jnp.exp(log_a)
    mult = jnp.sqrt(-jnp.expm1(2.0 * log_a))
    mult = jnp.where(jnp.arange(T)[None, :, None] == 0, 1.0, mult)
    u = mult * (i * xf)

    def combine(left, right):
        a_l, b_l = left
        a_r, b_r = right
        return a_l * a_r, a_r * b_l + b_r

    _, h = lax.associative_scan(combine, (a, u), axis=1)
    return h.astype(x.dtype)


def gla_chunked(q, k, logf, v):
    B, T, H, K = q.shape
    V = v.shape[-1]
    C = HG_CHUNK
    N = T // C

    def chunks(t):
        return t.reshape(B, N, C, H, t.shape[-1]).transpose(0, 3, 1, 2, 4)

    q, k, logf, v = chunks(q), chunks(k), chunks(logf), chunks(v)
    b = jnp.cumsum(logf, axis=3)
    b_last = b[:, :, :, -1:, :]
    qe = q * jnp.exp(b)
    ke = k * jnp.exp(-b)
    scores = jnp.einsum('bhnck,bhnsk->bhncs', qe, ke)
    causal = jnp.tril(jnp.ones((C, C), dtype=bool))
    scores = jnp.where(causal, scores, 0.0)
    o_intra = jnp.einsum('bhncs,bhnsv->bhncv', scores, v)

    kd = k * jnp.exp(b_last - b)
    dS = jnp.einsum('bhnsk,bhnsv->bhnkv', kd, v)
    decay = jnp.exp(b_last[:, :, :, 0, :])

    def step(S, inp):
        d, ds = inp
        return d[..., None] * S + ds, S

    S0 = jnp.zeros((B, H, K, V), jnp.float32)
    _, S_prev = lax.scan(step, S0, (jnp.moveaxis(decay, 2, 0), jnp.moveaxis(dS, 2, 0)))
    o_inter = jnp.einsum('bhnck,nbhkv->bhncv', qe, S_prev)
    o = o_intra + o_inter
    return o.transpose(0, 2, 3, 1, 4).reshape(B, T, H, V)


def hgrn2_branch(q, fz, iv, g, lb, norm_w):
    B, T, _ = q.shape
    qf, fzf, ivf, gf = (t.astype(jnp.float32) for t in (q, fz, iv, g))
    lb = lb.astype(jnp.float32)
    f = lb + (1.0 - lb) * jax.nn.sigmoid(fzf)
    logf = jnp.log(f)
    k = (1.0 - lb) * jax.nn.sigmoid(-fzf)
    qs = jax.nn.silu(qf) * (HG_HEAD_DIM ** -0.5)
    heads = lambda t: t.reshape(B, T, HG_HEADS, HG_HEAD_DIM)
    o = gla_chunked(heads(qs), heads(k), heads(logf), heads(ivf))
    o = o * lax.rsqrt(jnp.mean(o * o, axis=-1, keepdims=True) + EPS) * norm_w.astype(jnp.float32)
    o = o.reshape(B, T, D_HG) * jax.nn.silu(gf)
    return o.astype(q.dtype)


def setup_inputs(seed: int = 0) -> dict:
    key = jax.random.key(seed)
    ks = jax.random.split(key, 20)
    f32 = jnp.float32
    nrm = lambda k, shape, scale: scale * jax.random.normal(k, shape, f32)
    u = jax.random.uniform(ks[10], (DEPTH, D_RG), f32, minval=0.9, maxval=0.999)
    s = u ** (1.0 / RG_C)
    rg_lambda = jnp.log(s) - jnp.log1p(-s)
    return {
        "x": jax.random.normal(ks[0], (BATCH, SEQ, D_MODEL), f32),
        "p": jax.random.normal(ks[1], (DEPTH, BATCH, SEQ, PLE_DIM), f32),
        "norm_mix_w": 1.0 + nrm(ks[2], (DEPTH, D_MODEL), 0.1),
        "w_in": nrm(ks[3], (DEPTH, D_MODEL, D_IN), D_MODEL ** -0.5),
        "conv_w": nrm(ks[4], (DEPTH, CONV_WIDTH, D_RG), CONV_WIDTH ** -0.5),
        "conv_b": nrm(ks[5], (DEPTH, D_RG), 0.01),
        "rg_wa": nrm(ks[6], (DEPTH, RG_BLOCKS, RG_BLOCK, RG_BLOCK), RG_BLOCK ** -0.5),
        "rg_ba": nrm(ks[7], (DEPTH, D_RG), 0.1),
        "rg_wx": nrm(ks[8], (DEPTH, RG_BLOCKS, RG_BLOCK, RG_BLOCK), RG_BLOCK ** -0.5),
        "rg_bx": nrm(ks[9], (DEPTH, D_RG), 0.1),
        "rg_lambda": rg_lambda,
        "hg_lb": nrm(ks[11], (DEPTH + 1, D_HG), 0.1),
        "hg_norm_w": 1.0 + nrm(ks[12], (DEPTH, HG_HEAD_DIM), 0.1),
        "w_out": nrm(ks[13], (DEPTH, D_MIX, D_MODEL), D_MIX ** -0.5),
        "ple_norm_w": 1.0 + nrm(ks[14], (DEPTH, D_MODEL), 0.1),
        "w_ple_gate": nrm(ks[15], (DEPTH, D_MODEL, D_MODEL), D_MODEL ** -0.5),
        "b_ple_gate": nrm(ks[16], (DEPTH, D_MODEL), 0.1),
        "w_ple_proj": nrm(ks[17], (DEPTH, PLE_DIM, D_MODEL), PLE_DIM ** -0.5),
        "final_norm_w": 1.0 + nrm(ks[18], (D_MODEL,), 0.1),
    }


def reference(x, p, norm_mix_w, w_in, conv_w, conv_b, rg_wa, rg_ba, rg_wx, rg_bx,
              rg_lambda, hg_lb, hg_norm_w, w_out, ple_norm_w, w_ple_gate, b_ple_gate,
              w_ple_proj, final_norm_w):
    lb_all = jnp.cumsum(jax.nn.softmax(hg_lb.astype(jnp.float32), axis=0), axis=0)
    split_at = [D_RG, 2 * D_RG, 2 * D_RG + D_HG, 2 * D_RG + 2 * D_HG, 2 * D_RG + 3 * D_HG]
    h = x
    for l in range(DEPTH):
        u = rms_norm(h, norm_mix_w[l])
        proj = u @ w_in[l]
        xa, ga, qb, fb, ib, gb = jnp.split(proj, split_at, axis=-1)
        xa = causal_depthwise_conv(xa, conv_w[l], conv_b[l])
        ya = rg_lru(xa, rg_wa[l], rg_ba[l], rg_wx[l], rg_bx[l], rg_lambda[l]) * jax.nn.silu(ga)
        yb = hgrn2_branch(qb, fb, ib, gb, lb_all[l], hg_norm_w[l])
        h = h + jnp.concatenate([ya, yb], axis=-1) @ w_out[l]
        gate = jax.nn.sigmoid(rms_norm(h, ple_norm_w[l]) @ w_ple_gate[l] + b_ple_gate[l])
        h = h + gate * (p[l] @ w_ple_proj[l])
    return rms_norm(h, final_norm_w)
```

```python
import contextlib
import heapq
import numpy as np
import concourse.bass as bass
import concourse.mybir as mybir
from concourse.bass_utils import run_bass_kernel_spmd

F32 = mybir.dt.float32
BF16 = mybir.dt.bfloat16
AF = mybir.ActivationFunctionType
ALU = mybir.AluOpType
AX = mybir.AxisListType

ENGS = ("PE", "ACT", "DVE", "POOL", "SP")
ACT_WAIT = 0.25
USE_BL = False
USE_BL_PE = True
BL_LAT = 1.5
BL_ENGS = ("PE", "DVE", "POOL", "ACT")
EVAC_FIRST = True
PESSIMISM = 1.5
SWITCH_COST = 2.0
HOLD = 0.0
MIN_BATCH = 1


class Op:
    __slots__ = ("id", "eng", "fn", "dur", "aset", "deps", "name", "pos", "sig",
                 "sigval", "is_dma", "dsem", "dval", "lat", "users", "t_end", "t_start", "pri")

    def __init__(self, id, eng, fn, dur, aset, name, is_dma=False, lat=0.0):
        self.id = id
        self.eng = eng
        self.fn = fn
        self.dur = dur
        self.aset = aset
        self.deps = {}
        self.name = name
        self.pos = -1
        self.sig = False
        self.sigval = 0
        self.is_dma = is_dma
        self.dsem = None
        self.dval = 0
        self.lat = lat
        self.users = []
        self.t_end = 0.0
        self.pri = 1


class Prog:
    N_DMA_SEMS = 24

    def __init__(self, nc):
        self.nc = nc
        self.ops = []
        self.last_w = {}
        self.readers = {}
        self.ndma = 0
        self.dma_ops = []
        self.tag = ""

    def _add(self, op, reads, writes):
        for k in reads:
            for d in self.last_w.get(k, ()):
                op.deps[d] = True
        for k in writes:
            for d in self.last_w.get(k, ()):
                op.deps.setdefault(d, False)
            for d in self.readers.get(k, ()):
                if d != op.id:
                    op.deps.setdefault(d, False)
        for k in reads:
            self.readers.setdefault(k, []).append(op.id)
        for k in writes:
            self.last_w[k] = [op.id]
            self.readers[k] = []
        op.deps.pop(op.id, None)
        self.ops.append(op)
        return op

    def op(self, eng, fn, reads=(), writes=(), dur=0.3, aset=None, name=""):
        o = Op(len(self.ops), eng, fn, dur, aset, name or self.tag)
        if EVAC_FIRST and eng in ("ACT", "DVE") and any(isinstance(k, tuple) and str(k[0]).startswith("p") and "_" in str(k[0])
                                                      and str(k[0]).split("_")[0] in ("pa", "pb", "pc", "pn") for k in reads):
            o.pri = 0
        return self._add(o, reads, writes)

    def dma(self, out, in_, reads=(), writes=(), nbytes=0, name="dma"):
        def fn(e, out=out, in_=in_):
            return e.dma_start(out=out, in_=in_)
        o = Op(len(self.ops), "SP", fn, 0.08, None, self.tag, is_dma=True,
               lat=2.0 + nbytes / 250e3)
        self.ndma += 1
        self.dma_ops.append(o)
        return self._add(o, reads, writes)

    def schedule(self):
        ops = self.ops
        n = len(ops)
        for o in ops:
            o.users = []
        indeg = [0] * n
        for o in ops:
            for d in o.deps:
                ops[d].users.append(o.id)
            indeg[o.id] = len(o.deps)
        bl = [0.0] * n
        for o in reversed(ops):
            m = 0.0
            for u in o.users:
                c = bl[u] + (BL_LAT if ops[u].eng != o.eng else 0.0) + (o.lat if o.is_dma else 0.0)
                if c > m:
                    m = c
            bl[o.id] = o.dur * (1.0 if o.eng in ("PE", "SP") else PESSIMISM) + m
        self.bl = bl
        ready = {e: [] for e in ENGS}
        ready_t = [0.0] * n
        for o in ops:
            if indeg[o.id] == 0:
                heapq.heappush(ready[o.eng], o.id)
        free_t = {e: 0.0 for e in ENGS}
        order = {e: [] for e in ENGS}
        cur_set = None
        done = 0
        slot_last = [None] * self.N_DMA_SEMS
        slot_end = [0.0] * self.N_DMA_SEMS
        slot_cnt = [0] * self.N_DMA_SEMS
        SEM_LAT = 0.8
        while done < n:
            best = None
            for e in ENGS:
                if not ready[e]:
                    continue
                cands = heapq.nsmallest(16, ready[e])
                ft = free_t[e]
                pick = None
                pick_key = None
                n_other = 0
                if e == "ACT" and cur_set is not None:
                    for cid in cands:
                        if ops[cid].aset is not None and ops[cid].aset != cur_set and ready_t[cid] <= ft + 0.25:
                            n_other += 1
                for cid in cands:
                    st = max(ft, ready_t[cid])
                    sw = (e == "ACT" and ops[cid].aset is not None and cur_set is not None
                          and ops[cid].aset != cur_set)
                    if sw and n_other < MIN_BATCH:
                        st = max(st, ready_t[cid] + HOLD)
                    late = st > ft + (0.25 if (sw or e != "ACT") else ACT_WAIT)
                    pr = (ops[cid].pri, -bl[cid] if (USE_BL or (USE_BL_PE and e in BL_ENGS)) else cid)
                    if not late:
                        key = (0, 1 if sw else 0, pr, st)
                    else:
                        key = (1, st + (SWITCH_COST if sw else 0.0), pr, st)
                    if pick_key is None or key < pick_key:
                        pick_key = key
                        pick = cid
                pick_key = (pick_key[3],)
                st = pick_key[0]
                if best is None or (st, pick) < (best[0], best[2]):
                    best = (st, e, pick)
            st, e, cid = best
            o = ops[cid]
            ready[e].remove(cid)
            heapq.heapify(ready[e])
            if e == "ACT" and o.aset is not None:
                if cur_set is not None and o.aset != cur_set:
                    st += SWITCH_COST
                    self.n_switch = getattr(self, "n_switch", 0) + 1
                cur_set = o.aset
            if o.is_dma:
                sl = min(range(self.N_DMA_SEMS), key=lambda i: slot_end[i])
                if slot_last[sl] is not None:
                    o.deps.setdefault(slot_last[sl], False)
                    st = max(st, slot_end[sl] + SEM_LAT)
                slot_cnt[sl] += 1
                o.dsem = sl
                o.dval = 16 * slot_cnt[sl]
                slot_last[sl] = cid
                slot_end[sl] = st + o.dur + o.lat
            o.pos = len(order[e])
            order[e].append(cid)
            end_issue = st + o.dur * (1.0 if e in ("PE", "SP") else PESSIMISM)
            free_t[e] = end_issue
            o.t_end = end_issue + (o.lat if o.is_dma else 0.0)
            o.t_start = st
            done += 1
            for u in o.users:
                indeg[u] -= 1
                rt = o.t_end + (SEM_LAT if (ops[u].eng != e or o.is_dma) else 0.0)
                if rt > ready_t[u]:
                    ready_t[u] = rt
                if indeg[u] == 0:
                    heapq.heappush(ready[ops[u].eng], u)
        self.order = order
        self.est_time = max(o.t_end for o in ops)
        return order

    def emit(self, block, sems, dma_sems):
        ops = self.ops
        order = self.order
        SAME_ENG_DIST = 3

        def needs_wait(o, d, is_raw):
            dop = ops[d]
            if dop.is_dma:
                return True
            if dop.eng != o.eng:
                return True
            if o.eng in ("PE", "SP"):
                return False
            return True

        for o in ops:
            for d, is_raw in o.deps.items():
                if needs_wait(o, d, is_raw):
                    ops[d].sig = True
        for e in ENGS:
            c = 0
            for cid in order[e]:
                o = ops[cid]
                if o.is_dma:
                    continue
                if o.sig:
                    c += 1
                    o.sigval = c
        clock = [None] * len(ops)
        eng_known = {e: {} for e in ENGS}
        waits = {}
        ptr = {e: 0 for e in ENGS}
        remaining = len(ops)

        def semkey_of(dop):
            return ("d", dop.dsem) if dop.is_dma else ("e", dop.eng)

        def val_of(dop):
            return dop.dval if dop.is_dma else dop.sigval

        while remaining:
            progressed = False
            for e in ENGS:
                while ptr[e] < len(order[e]):
                    o = ops[order[e][ptr[e]]]
                    if any(clock[d] is None for d in o.deps):
                        break
                    known = eng_known[e]
                    wl = []
                    need = [d for d, r in o.deps.items() if needs_wait(o, d, r)]
                    need.sort(key=lambda d: -val_of(ops[d]))
                    for d in need:
                        dop = ops[d]
                        sk = semkey_of(dop)
                        v = val_of(dop)
                        if known.get(sk, 0) >= v:
                            continue
                        wl.append((sk, v))
                        for k2, v2 in clock[d].items():
                            if known.get(k2, 0) < v2:
                                known[k2] = v2
                    waits[o.id] = wl
                    ck = dict(known)
                    if o.is_dma:
                        ck[("d", o.dsem)] = o.dval
                    elif o.sig:
                        ck[("e", e)] = o.sigval
                        known[("e", e)] = max(known.get(("e", e), 0), 0)
                    clock[o.id] = ck
                    ptr[e] += 1
                    remaining -= 1
                    progressed = True
            assert progressed, "scheduler deadlock"

        def semh(sk):
            return dma_sems[sk[1]] if sk[0] == "d" else sems[sk[1]]

        self.n_waits = sum(len(w) for w in waits.values())

        def run_engine(e):
            def body(eng):
                for cid in order[e]:
                    o = ops[cid]
                    for sk, v in waits[cid]:
                        eng.wait_ge(semh(sk), v)
                    ins = o.fn(eng)
                    if ins is None:
                        continue
                    if o.is_dma:
                        ins.then_inc(dma_sems[o.dsem], 16)
                    elif o.sig:
                        ins.then_inc(sems[e], 1)
            return body

        block.tensor(run_engine("PE"))
        block.scalar(run_engine("ACT"))
        block.vector(run_engine("DVE"))
        block.gpsimd(run_engine("POOL"))
        block.sync(run_engine("SP"))


class Buf:
    def __init__(self, t, name):
        self.t = t
        self.name = name

    def __getitem__(self, idx):
        return self.t[idx]

    def k(self, part=None):
        return (self.name, part)


class Ring:
    def __init__(self, bufs):
        self.bufs = bufs
        self.i = 0

    def next(self):
        b = self.bufs[self.i % len(self.bufs)]
        self.i += 1
        return b


D = 1024
DIN = 3072
PLE = 256
TT = 512
SUB = 128
NV = 64
EPS = 1e-6
HS = 128 ** -0.5

V_NMW, V_PNW, V_CW, V_CB, V_BA, V_BX, V_LAM, V_LB0, V_LB1, V_NW = 0, 8, 16, 32, 36, 40, 44, 48, 52, 56
C_HCL, C_CL, C_HBA, C_HBX, C_FS, C_FB, C_KN, C_KP, C_E, C_Y, C_Y2, C_PL, C_TH, C_M05, C_ZERO = \
    0, 4, 8, 12, 16, 20, 24, 28, 32, 36, 40, 44, 48, 52, 53


def act_d(n):
    return 0.16 + n / 1200.0


def dve_d(n, mode=1.0):
    return (n / mode + 151) / 960.0


def pts_d(n):
    return 0.12 + n / 1050.0


def ptt_d(n):
    return 0.15 + n * 2.25 / 1000.0


def pe_d(n):
    return 0.045 + max(n, 64) / 2400.0


def A(out, in_, func, bias=0.0, scale=1.0, accum=None):
    def f(e):
        if accum is not None:
            return e.activation(out=out, in_=in_, func=func, bias=bias, scale=scale, accum_out=accum)
        return e.activation(out=out, in_=in_, func=func, bias=bias, scale=scale)
    return f


def TS(out, in0, s1, s2, op0, op1):
    return lambda e: e.tensor_scalar(out=out, in0=in0, scalar1=s1, scalar2=s2, op0=op0, op1=op1)


def TTo(out, in0, in1, op):
    return lambda e: e.tensor_tensor(out=out, in0=in0, in1=in1, op=op)


def STT(out, in0, scalar, in1, op0, op1):
    return lambda e: e.scalar_tensor_tensor(out=out, in0=in0, scalar=scalar, in1=in1, op0=op0, op1=op1)


def CP(out, in_):
    return lambda e: e.tensor_copy(out=out, in_=in_)


def MS(out, v):
    return lambda e: e.memset(out, v)


def MM(items):
    def f(e):
        r = None
        for (out, lhsT, rhs, st, sp) in items:
            r = e.matmul(out, lhsT=lhsT, rhs=rhs, start=st, stop=sp)
        return r
    return f


def build_program(T):
    NT = T // TT
    nc = bass.Bass("TRN2", target_bir_lowering=False)

    def din(name, shape):
        return nc.dram_tensor(name, shape, F32, kind="ExternalInput").ap()

    x_d = din("x", [T, D])
    p_d = din("p", [T, PLE])
    w_in_d = din("w_in", [D, DIN])
    w_out_d = din("w_out", [D, D])
    w_gate_d = din("w_gate", [D, D])
    w_pp_d = din("w_pp", [PLE, D])
    vecs_d = din("vecs", [128, NV])
    fnw_d = din("fnw", [128, D])
    bg_d = din("bg", [1, D])
    wabd_d = din("wabd", [128, 512])
    wxbd_d = din("wxbd", [128, 512])
    ident_d = din("ident", [128, 128])
    mask_d = din("mask", [128, 128])
    out_d = nc.dram_tensor("out", [T, D], F32, kind="ExternalOutput").ap()

    with contextlib.ExitStack() as es:
        cnt = [0]

        sbytes = [0]

        def sb(name, shape, dt=F32):
            cnt[0] += 1
            nm = f"{name}_{cnt[0]}"
            sbytes[0] += int(np.prod(shape[1:])) * (2 if dt == BF16 else 4)
            return Buf(es.enter_context(nc.sbuf_tensor(nm, shape, dt)), nm)

        def psb(name):
            cnt[0] += 1
            nm = f"{name}_{cnt[0]}"
            b = Buf(es.enter_context(nc.psum_tensor(nm, [128, 512], F32)), nm)
            b.bf = b.t[:, :].bitcast(BF16)
            return b

        def ring(name, n, shape, dt=F32):
            return Ring([sb(name, shape, dt) for _ in range(n)])

        w_in_bf = sb("w_in_bf", [128, 8, DIN], BF16)
        w_out_bf = sb("w_out_bf", [128, 8, D], BF16)
        w_gate_bf = sb("w_gate_bf", [128, 8, D], BF16)
        w_pp_bf = sb("w_pp_bf", [128, 2, D], BF16)
        vecs = sb("vecs", [128, NV])
        dv = sb("dv", [128, 64])
        ident_bf = sb("ident_bf", [128, 128], BF16)
        mask_f = sb("mask_f", [128, 128])
        fnw = sb("fnw", [128, D])
        bg_bf = sb("bg_bf", [1, D], BF16)
        ones_bf = sb("ones_bf", [1, 128], BF16)
        wabd_bf = sb("wabd_bf", [128, 512], BF16)
        wxbd_bf = sb("wxbd_bf", [128, 512], BF16)
        halo = sb("halo", [128, 4, 3])
        hlast = sb("hlast", [128, 4])
        Z = sb("Z", [128, 4, 128])
        xs_ring = ring("xs", 2, [128, D])
        xr_ring = ring("xr", 2, [128, D])
        tg_ring = ring("tg", 1, [128, D])
        xn_ring = ring("xn", 2, [128, D], BF16)
        hn_ring = ring("hn", 2, [128, D], BF16)
        uT_ring = ring("uT", 2, [128, 8, TT], BF16)
        xa_ring = ring("xa", 2, [128, TT + 3])
        xcb_ring = ring("xcb", 2, [128, TT], BF16)
        c2k = ring("c2k", 10, [128, TT])
        c1k = ring("c1k", 3, [128, TT], BF16)
        yaT_ring = ring("yaT", 2, [128, 4, TT], BF16)
        ybT_ring = ring("ybT", 2, [128, 4, TT], BF16)
        v_ring = ring("v_bf", 2, [128, TT], BF16)
        gs_ring = ring("gs_bf", 2, [128, TT], BF16)
        qeT = sb("qeT", [128, 4, TT], BF16)
        keT = sb("keT", [128, 4, TT], BF16)
        ket_ring = ring("ket", 1, [128, TT], BF16)
        scb_ring = ring("scb", 1, [128, TT], BF16)
        ybt_ring = ring("ybt", 1, [128, TT], BF16)
        W_ring = ring("W", 2, [128, 4, 128], BF16)
        ds_ring = ring("dsv", 2, [128, 4, 4])
        st_n1 = ring("stn1", 4, [128, 12])
        st_gla = ring("stgla", 2, [128, 12])
        st_sc = ring("stsc", 4, [128, 12])
        hnT_ring = ring("hnT", 1, [128, 8, 128], BF16)
        p_ring = ring("psub", 2, [128, PLE])
        pbf_ring = ring("pbf", 1, [128, PLE], BF16)
        pT_ring = ring("pT", 1, [128, 2, 128], BF16)
        pn = Ring([psb("pn") for _ in range(1)])
        pa = Ring([psb("pa") for _ in range(2)])
        pb = Ring([psb("pb") for _ in range(3)])
        pc = Ring([psb("pc") for _ in range(2)])

        sems = {e: es.enter_context(nc.semaphore(f"s_{e}")) for e in ENGS}
        dsems = [es.enter_context(nc.semaphore(f"d_{i}")) for i in range(Prog.N_DMA_SEMS)]
        block = es.enter_context(nc.Block())
        P = Prog(nc)

        def vcol(c):
            return vecs[:, c:c + 1]

        def dcol(c):
            return dv[:, c:c + 1]

        def TR(items):
            def f(e):
                r = None
                for (out, in_) in items:
                    r = e.transpose(out=out, in_=in_, identity=ident_bf[:, :])
                return r
            return f

        P.dma(vecs[:, :], vecs_d, writes=[vecs.k()], nbytes=128 * NV * 4)
        P.dma(mask_f[:, :], mask_d, writes=[mask_f.k()], nbytes=65536)
        P.dma(fnw[:, :], fnw_d, writes=[fnw.k()], nbytes=128 * D * 4)
        P.op("POOL", MS(halo[:, :, :], 0.0), writes=[halo.k(j) for j in range(4)])
        P.op("POOL", MS(hlast[:, :], 0.0), writes=[hlast.k(j) for j in range(4)])
        P.op("POOL", MS(Z[:, :, :], 0.0), writes=[Z.k()])
        P.op("POOL", MS(ones_bf[:, :], 1.0), writes=[ones_bf.k()])
        W0 = W_ring.next()
        P.op("POOL", MS(W0[:, :, :], 0.0), writes=[W0.k()])
        ds_prev = ds_ring.next()
        P.op("POOL", MS(ds_prev[:, :, :], 0.0), writes=[ds_prev.k(h) for h in range(4)])
        pre_x = []
        for s in range(2):
            xb_ = xs_ring.next()
            P.dma(xb_[:, :], x_d[s * SUB:(s + 1) * SUB, :], writes=[xb_.k()], nbytes=128 * D * 4)
            pre_x.append(xb_)

        dk = [dv.k()]
        vk = [vecs.k()]

        def dts(out_c, n, in_ap, s1, s2, op0=ALU.mult, op1=ALU.add, eng="DVE"):
            P.op(eng, TS(dv[:, out_c:out_c + n], in_ap, s1, s2, op0, op1), reads=dk + vk, writes=dk, dur=0.2)

        def dtt(out_c, n, a_ap, b_ap, op, eng="DVE"):
            P.op(eng, TTo(dv[:, out_c:out_c + n], a_ap, b_ap, op), reads=dk + vk, writes=dk, dur=0.2)

        P.op("POOL", MS(dv[:, :], 0.0), writes=dk)
        P.op("POOL", MS(dv[:, C_M05:C_M05 + 1], -0.5), reads=dk, writes=dk)
        dts(C_HBA, 4, vecs[:, V_BA:V_BA + 4], 0.5, 0.0)
        dts(C_HBX, 4, vecs[:, V_BX:V_BX + 4], 0.5, 0.0)
        P.op("ACT", A(dv[:, C_E:C_E + 4], vecs[:, V_LAM:V_LAM + 4], AF.Exp, scale=-1.0),
             reads=dk + vk, writes=dk, dur=0.4, aset="lnexp")
        dts(C_Y, 4, dv[:, C_E:C_E + 4], 1.0, 2.0)
        P.op("DVE", lambda e: e.reciprocal(out=dv[:, C_Y:C_Y + 4], in_=dv[:, C_Y:C_Y + 4]), reads=dk, writes=dk, dur=0.2)
        dtt(C_Y, 4, dv[:, C_Y:C_Y + 4], dv[:, C_E:C_E + 4], ALU.mult)
        dtt(C_Y2, 4, dv[:, C_Y:C_Y + 4], dv[:, C_Y:C_Y + 4], ALU.mult)
        dts(C_PL, 4, dv[:, C_Y2:C_Y2 + 4], 1.0 / 11.0, 1.0 / 9.0)
        for cst in (1.0 / 7.0, 1.0 / 5.0, 1.0 / 3.0, 1.0):
            dtt(C_PL, 4, dv[:, C_PL:C_PL + 4], dv[:, C_Y2:C_Y2 + 4], ALU.mult)
            dts(C_PL, 4, dv[:, C_PL:C_PL + 4], 1.0, cst)
        dtt(C_PL, 4, dv[:, C_PL:C_PL + 4], dv[:, C_Y:C_Y + 4], ALU.mult)
        dts(C_CL, 4, dv[:, C_PL:C_PL + 4], -16.0, 0.0)
        dts(C_HCL, 4, dv[:, C_PL:C_PL + 4], -8.0, 0.0)
        dtt(C_TH, 4, vecs[:, V_LB0:V_LB0 + 4], vecs[:, V_LB1:V_LB1 + 4], ALU.subtract)
        P.op("ACT", A(dv[:, C_TH:C_TH + 4], dv[:, C_TH:C_TH + 4], AF.Tanh, scale=0.5),
             reads=dk, writes=dk, dur=0.4, aset="silu")
        dts(C_FS, 4, dv[:, C_TH:C_TH + 4], -0.25, 0.25)
        dts(C_FB, 4, dv[:, C_TH:C_TH + 4], 0.25, 0.75)
        dts(C_KP, 4, dv[:, C_TH:C_TH + 4], -0.25 * HS, 0.25 * HS)
        dts(C_KN, 4, dv[:, C_TH:C_TH + 4], 0.25 * HS, -0.25 * HS)

        stage_bufs = xr_ring.bufs + tg_ring.bufs
        sidx = [0]

        def stage():
            b = stage_bufs[sidx[0] % len(stage_bufs)]
            sidx[0] += 1
            return b

        sgb = stage()
        P.dma(sgb[:, 0:128], ident_d, writes=[sgb.k()], nbytes=65536)
        P.op("DVE", CP(ident_bf[:, :], sgb[:, 0:128]), reads=[sgb.k()], writes=[ident_bf.k()], dur=0.3)
        sgb = stage()
        P.dma(sgb[:, 0:512], wabd_d, writes=[sgb.k()], nbytes=128 * 512 * 4)
        P.op("DVE", CP(wabd_bf[:, :], sgb[:, 0:512]), reads=[sgb.k()], writes=[wabd_bf.k()], dur=dve_d(512, 2))
        sgb = stage()
        P.dma(sgb[:, 0:512], wxbd_d, writes=[sgb.k()], nbytes=128 * 512 * 4)
        P.op("DVE", CP(wxbd_bf[:, :], sgb[:, 0:512]), reads=[sgb.k()], writes=[wxbd_bf.k()], dur=dve_d(512, 2))
        sgb = stage()
        P.dma(sgb[0:1, :], bg_d, writes=[sgb.k()], nbytes=4096)
        P.op("DVE", CP(bg_bf[:, :], sgb[0:1, :]), reads=[sgb.k()], writes=[bg_bf.k()], dur=dve_d(1024, 2))
        w_in_v = w_in_d.rearrange("(kc p) n -> p kc n", p=128)
        for ci, cc in enumerate([0, 4, 1, 5, 2, 6, 3, 7, 12, 8, 13, 9, 14, 10, 15, 11] + list(range(16, 24))):
            sgb = stage()
            sv = sgb[:, :].rearrange("p (k n) -> p k n", k=8)
            P.dma(sv, w_in_v[:, :, cc * 128:(cc + 1) * 128], writes=[sgb.k()], nbytes=128 * 1024 * 4)
            eng = "DVE" if ci % 2 == 0 else "POOL"
            P.op(eng, TTo(w_in_bf[:, :, cc * 128:(cc + 1) * 128], sv,
                          vecs[:, V_NMW:V_NMW + 8].unsqueeze(2).broadcast_to([128, 8, 128]), ALU.mult),
                 reads=[sgb.k(), vecs.k()], writes=[("w_in", cc)],
                 dur=dve_d(1024) if eng == "DVE" else ptt_d(1024))
        def late_weights():
            for kc in range(8):
                sgb = stage()
                P.dma(sgb[:, :], w_out_d[kc * 128:(kc + 1) * 128, :], writes=[sgb.k()], nbytes=128 * 1024 * 4)
                if kc < 4:
                    P.op("DVE", CP(w_out_bf[:, kc, :], sgb[:, :]), reads=[sgb.k()], writes=[("w_out", kc)], dur=dve_d(1024, 2))
                else:
                    P.op("ACT", A(w_out_bf[:, kc, :], sgb[:, :], AF.Identity, scale=vcol(V_NW)),
                         reads=[sgb.k(), vecs.k()], writes=[("w_out", kc)], dur=act_d(1024))
            for kc in range(8):
                sgb = stage()
                P.dma(sgb[:, :], w_gate_d[kc * 128:(kc + 1) * 128, :], writes=[sgb.k()], nbytes=128 * 1024 * 4)
                P.op("ACT", A(w_gate_bf[:, kc, :], sgb[:, :], AF.Identity, scale=vcol(V_PNW + kc)),
                     reads=[sgb.k(), vecs.k()], writes=[("w_gate", kc)], dur=act_d(1024))
            for kc in range(2):
                sgb = stage()
                P.dma(sgb[:, :], w_pp_d[kc * 128:(kc + 1) * 128, :], writes=[sgb.k()], nbytes=128 * 1024 * 4)
                P.op("DVE", CP(w_pp_bf[:, kc, :], sgb[:, :]), reads=[sgb.k()], writes=[("w_pp", kc)], dur=dve_d(1024, 2))

        W_in_all = [("w_in", cc) for cc in range(24)]

        def rstd_ops(st, c_in, c_tmp, c_out, n, inv_n):
            P.op("POOL", TS(st[:, c_tmp:c_tmp + n], st[:, c_in:c_in + n], inv_n, EPS, ALU.mult, ALU.add),
                 reads=[st.k()], writes=[st.k()], dur=0.35)
            P.op("POOL", TTo(st[:, c_out:c_out + n], st[:, c_tmp:c_tmp + n],
                             dv[:, C_M05:C_M05 + 1].broadcast_to([128, n]), ALU.pow),
                 reads=[st.k(), dv.k()], writes=[st.k()], dur=0.75)

        W_cur = W0
        out_keys = []
        for t in range(NT):
            r_t = t * TT
            uT = uT_ring.next()
            yaT = yaT_ring.next()
            ybT = ybT_ring.next()
            P.tag = f"t{t}:N1"
            for s in range(4):
                if t == 0 and s < 2:
                    xs = pre_x[s]
                else:
                    xs = xs_ring.next()
                    P.dma(xs[:, :], x_d[r_t + s * SUB:r_t + (s + 1) * SUB, :], writes=[xs.k()], nbytes=128 * D * 4)
                xn = xn_ring.next()
                st = st_n1.next()
                P.op("ACT", A(xn[:, :], xs[:, :], AF.Square, accum=st[:, 0:1]),
                     reads=[xs.k()], writes=[xn.k(), st.k()], dur=act_d(1024))
                rstd_ops(st, 0, 1, 2, 1, 1.0 / D)
                P.op("POOL", TS(xn[:, :], xs[:, :], st[:, 2:3], 0.0, ALU.mult, ALU.add),
                     reads=[xs.k(), st.k()], writes=[xn.k()], dur=pts_d(1024))
                bank = pn.next()
                P.op("PE", TR([(bank.bf[:, kc * 128:(kc + 1) * 128], xn[:, kc * 128:(kc + 1) * 128]) for kc in range(8)]),
                     reads=[xn.k(), ident_bf.k()], writes=[bank.k()], dur=8 * pe_d(128))
                P.op("DVE", CP(uT[:, :, s * SUB:(s + 1) * SUB], bank.bf[:, :].rearrange("p (k n) -> p k n", k=8)),
                     reads=[bank.k()], writes=[uT.k(s)], dur=dve_d(1024, 2))
            uT_keys = [uT.k(s) for s in range(4)]

            def inproj_cm(cc):
                bank = pa.next()
                P.op("PE", MM([(bank[:, :], w_in_bf[:, kc, cc * 128:(cc + 1) * 128], uT[:, kc, :], kc == 0, kc == 7)
                               for kc in range(8)]),
                     reads=uT_keys + [("w_in", cc)], writes=[bank.k()], dur=8 * pe_d(512))
                return bank

            P.tag = f"t{t}:GA"
            for j in range(4):
                bank = inproj_cm(j)
                xa = xa_ring.next()
                P.op("POOL", CP(xa[:, 0:3], halo[:, j, :]), reads=[halo.k(j)], writes=[xa.k()], dur=0.15)
                P.op("ACT", A(xa[:, 3:TT + 3], bank[:, :], AF.Copy), reads=[bank.k()], writes=[xa.k()], dur=act_d(512))
                P.op("POOL", CP(halo[:, j, :], xa[:, TT:TT + 3]), reads=[xa.k()], writes=[halo.k(j)], dur=0.15)
                xcf = c2k.next()
                xcb = xcb_ring.next()
                P.op("DVE", TS(xcf[:, :], xa[:, 3:TT + 3], vcol(V_CW + 3 * 4 + j), vcol(V_CB + j), ALU.mult, ALU.add),
                     reads=[xa.k(), vecs.k()], writes=[xcf.k()], dur=dve_d(512, 2))
                for tap in (2, 1, 0):
                    dst = xcb if tap == 0 else xcf
                    P.op("DVE", STT(dst[:, :], xa[:, tap:tap + TT], vcol(V_CW + tap * 4 + j), xcf[:, :], ALU.mult, ALU.add),
                         reads=[xa.k(), vecs.k(), xcf.k()], writes=[dst.k()], dur=dve_d(512))
                thr = c2k.next()
                thi = c2k.next()
                av = c2k.next()
                pr = pn.next()
                P.op("PE", MM([(pr[:, :], wabd_bf[:, j * 128:(j + 1) * 128], xcb[:, :], True, True)]),
                     reads=[xcb.k(), wabd_bf.k()], writes=[pr.k()], dur=pe_d(512))
                P.op("ACT", A(thr[:, :], pr[:, :], AF.Tanh, bias=dcol(C_HBA + j), scale=0.5),
                     reads=[pr.k(), dv.k()], writes=[thr.k()], dur=act_d(512), aset="silu")
                pi = pn.next()
                P.op("PE", MM([(pi[:, :], wxbd_bf[:, j * 128:(j + 1) * 128], xcb[:, :], True, True)]),
                     reads=[xcb.k(), wxbd_bf.k()], writes=[pi.k()], dur=pe_d(512))
                P.op("ACT", A(thi[:, :], pi[:, :], AF.Tanh, bias=dcol(C_HBX + j), scale=0.5),
                     reads=[pi.k(), dv.k()], writes=[thi.k()], dur=act_d(512), aset="silu")
                P.op("ACT", A(av[:, :], thr[:, :], AF.Exp, bias=dcol(C_HCL + j), scale=dcol(C_HCL + j)),
                     reads=[thr.k(), dv.k()], writes=[av.k()], dur=act_d(512), aset="lnexp")
                P.op("DVE", TTo(thr[:, :], av[:, :], av[:, :], ALU.mult),
                     reads=[av.k(), thr.k()], writes=[thr.k()], dur=dve_d(512))
                P.op("ACT", A(thr[:, :], thr[:, :], AF.Ln, bias=1.0, scale=-1.0),
                     reads=[thr.k()], writes=[thr.k()], dur=act_d(512), aset="lnexp")
                P.op("ACT", A(thr[:, :], thr[:, :], AF.Exp, bias=float(np.log(0.5)), scale=0.5),
                     reads=[thr.k()], writes=[thr.k()], dur=act_d(512), aset="lnexp")
                if t == 0:
                    P.op("DVE", MS(thr[:, 0:1], 0.5), reads=[thr.k()], writes=[thr.k()], dur=0.1)
                P.op("DVE", STT(thi[:, :], thi[:, :], 1.0, xcb[:, :], ALU.add, ALU.mult),
                     reads=[thi.k(), xcb.k()], writes=[thi.k()], dur=dve_d(512))
                P.op("POOL", TTo(thi[:, :], thi[:, :], thr[:, :], ALU.mult),
                     reads=[thi.k(), thr.k()], writes=[thi.k()], dur=ptt_d(512))
                hb = c2k.next()
                P.op("DVE", (lambda hb=hb, av=av, thi=thi, j=j: (lambda e: e.tensor_tensor_scan(
                    out=hb[:, :], data0=av[:, :], data1=thi[:, :], initial=hlast[:, j:j + 1],
                    op0=ALU.mult, op1=ALU.add)))(),
                    reads=[av.k(), thi.k(), hlast.k(j)], writes=[hb.k()], dur=dve_d(512))
                P.op("POOL", CP(hlast[:, j:j + 1], hb[:, TT - 1:TT]), reads=[hb.k()], writes=[hlast.k(j)], dur=0.15)
                bank2 = inproj_cm(4 + j)
                sga = c1k.next()
                P.op("ACT", A(sga[:, :], bank2[:, :], AF.Silu), reads=[bank2.k()], writes=[sga.k()],
                     dur=act_d(512), aset="silu")
                P.op("POOL", TTo(yaT[:, j, :], hb[:, :], sga[:, :], ALU.mult),
                     reads=[hb.k(), sga.k()], writes=[yaT.k(j)], dur=ptt_d(512))

            P.tag = f"t{t}:GB"
            dsv = ds_ring.next()
            for h in range(4):
                bank = inproj_cm(12 + h)
                thf = c2k.next()
                P.op("ACT", A(thf[:, :], bank[:, :], AF.Tanh, scale=0.5), reads=[bank.k()], writes=[thf.k()],
                     dur=act_d(512), aset="silu")
                fg = c2k.next()
                P.op("POOL", TS(fg[:, :], thf[:, :], dcol(C_FS + h), dcol(C_FB + h), ALU.mult, ALU.add),
                     reads=[thf.k(), dv.k()], writes=[fg.k()], dur=pts_d(512))
                P.op("POOL", TS(thf[:, :], thf[:, :], dcol(C_KN + h), dcol(C_KP + h), ALU.mult, ALU.add),
                     reads=[thf.k(), dv.k()], writes=[thf.k()], dur=pts_d(512))
                Pc = c2k.next()

                def scans(e, Pc=Pc, fg=fg):
                    r = None
                    for s in range(4):
                        r = e.tensor_tensor_scan(out=Pc[:, s * SUB:(s + 1) * SUB], data0=fg[:, s * SUB:(s + 1) * SUB],
                                                 data1=dv[:, C_ZERO:C_ZERO + 1].broadcast_to([128, SUB]), initial=1.0, op0=ALU.mult, op1=ALU.add)
                    return r
                P.op("DVE", scans, reads=[fg.k(), dv.k()], writes=[Pc.k()], dur=4 * dve_d(128))
                P.op("POOL", CP(dsv[:, h, :], Pc[:, SUB - 1::SUB]), reads=[Pc.k()], writes=[dsv.k(h)], dur=0.15)
                Pinv = fg
                P.op("ACT", A(Pinv[:, :], Pc[:, :], AF.Ln), reads=[Pc.k()], writes=[Pinv.k()],
                     dur=act_d(512), aset="lnexp")
                P.op("ACT", A(Pinv[:, :], Pinv[:, :], AF.Exp, scale=-1.0), reads=[Pinv.k()], writes=[Pinv.k()],
                     dur=act_d(512), aset="lnexp")
                bank2 = inproj_cm(8 + h)
                qs = c1k.next()
                P.op("ACT", A(qs[:, :], bank2[:, :], AF.Silu), reads=[bank2.k()], writes=[qs.k()],
                     dur=act_d(512), aset="silu")
                P.op("DVE", TTo(qeT[:, h, :], qs[:, :], Pc[:, :], ALU.mult),
                     reads=[qs.k(), Pc.k()], writes=[qeT.k(h)], dur=dve_d(512))
                P.op("POOL", TTo(keT[:, h, :], thf[:, :], Pinv[:, :], ALU.mult),
                     reads=[thf.k(), Pinv.k()], writes=[keT.k(h)], dur=ptt_d(512))
            qk_keys = [qeT.k(h) for h in range(4)] + [keT.k(h) for h in range(4)]
            ds_keys = [dsv.k(h) for h in range(4)]
            dsp_keys = [ds_prev.k(h) for h in range(4)]

            if t == 0:
                P.tag = "late_w"
                late_weights()
            P.tag = f"t{t}:GLA"
            for s in range(4):
                cs = slice(s * SUB, (s + 1) * SUB)
                v_bf = v_ring.next()
                gs_bf = gs_ring.next()
                for which in range(2):
                    c0 = 2048 + which * 512
                    bank = pb.next()
                    P.op("PE", MM([(bank[:, :], uT[:, kc, cs], w_in_bf[:, kc, c0:c0 + 512], kc == 0, kc == 7)
                                   for kc in range(8)]),
                         reads=[uT.k(s)] + [("w_in", 16 + which * 4 + i) for i in range(4)], writes=[bank.k()],
                         dur=8 * pe_d(512))
                    if which == 0:
                        P.op("ACT", A(v_bf[:, :], bank[:, :], AF.Copy), reads=[bank.k()], writes=[v_bf.k()], dur=act_d(512))
                    else:
                        P.op("ACT", A(gs_bf[:, :], bank[:, :], AF.Silu), reads=[bank.k()], writes=[gs_bf.k()],
                             dur=act_d(512), aset="silu")
                pk = pb.next()
                P.op("PE", TR([(pk.bf[:, h * 128:(h + 1) * 128], keT[:, h, cs]) for h in range(4)]),
                     reads=[keT.k(h) for h in range(4)] + [ident_bf.k()], writes=[pk.k()], dur=4 * pe_d(128))
                ket = ket_ring.next()
                P.op("ACT", A(ket[:, :], pk.bf[:, 0:512], AF.Copy), reads=[pk.k()], writes=[ket.k()], dur=act_d(512))
                psc = pb.next()
                P.op("PE", MM([(psc[:, h * 128:(h + 1) * 128], keT[:, h, cs], qeT[:, h, cs], True, True) for h in range(4)]),
                     reads=qk_keys, writes=[psc.k()], dur=4 * pe_d(128))
                scb = scb_ring.next()
                P.op("DVE", TTo(scb[:, :].rearrange("p (h c) -> p h c", h=4), psc[:, :].rearrange("p (h c) -> p h c", h=4),
                                mask_f[:, :].unsqueeze(1).broadcast_to([128, 4, 128]), ALU.mult),
                     reads=[psc.k(), mask_f.k()], writes=[scb.k()], dur=dve_d(512))
                pds = pb.next()
                P.op("PE", MM([(pds[:, h * 128:(h + 1) * 128], ket[:, h * 128:(h + 1) * 128], v_bf[:, h * 128:(h + 1) * 128], True, True)
                               for h in range(4)]),
                     reads=[ket.k(), v_bf.k()], writes=[pds.k()], dur=4 * pe_d(128))
                po = pb.next()
                items = []
                for h in range(4):
                    hb_ = slice(h * 128, (h + 1) * 128)
                    items.append((po[:, hb_], scb[:, hb_], v_bf[:, hb_], True, False))
                    items.append((po[:, hb_], qeT[:, h, cs], W_cur[:, h, :], False, True))
                P.op("PE", MM(items), reads=[scb.k(), v_bf.k(), W_cur.k()] + qk_keys, writes=[po.k()], dur=8 * pe_d(128))
                if s == 0:
                    dpb, dpi, dpk = ds_prev, 3, dsp_keys
                else:
                    dpb, dpi, dpk = dsv, s - 1, ds_keys
                for h in range(4):
                    P.op("DVE", STT(Z[:, h, :], Z[:, h, :], dpb[:, h, dpi:dpi + 1], pds[:, h * 128:(h + 1) * 128], ALU.mult, ALU.add),
                         reads=[Z.k(), pds.k()] + dpk, writes=[Z.k()], dur=dve_d(128))
                Wn = W_ring.next()
                for h in range(4):
                    P.op("POOL", TS(Wn[:, h, :], Z[:, h, :], dsv[:, h, s:s + 1], 0.0, ALU.mult, ALU.add),
                         reads=[Z.k()] + ds_keys, writes=[Wn.k()], dur=pts_d(128))
                W_cur = Wn
                st = st_gla.next()
                ybt = ybt_ring.next()
                for h in range(4):
                    P.op("ACT", A(ybt[:, h * 128:(h + 1) * 128], po[:, h * 128:(h + 1) * 128], AF.Square, accum=st[:, h:h + 1]),
                         reads=[po.k()], writes=[ybt.k(), st.k()], dur=act_d(128))
                rstd_ops(st, 0, 4, 8, 4, 1.0 / 128.0)
                for h in range(4):
                    hb_ = slice(h * 128, (h + 1) * 128)
                    P.op("DVE", STT(ybt[:, hb_], po[:, hb_], st[:, 8 + h:9 + h], gs_bf[:, hb_], ALU.mult, ALU.mult),
                         reads=[po.k(), st.k(), gs_bf.k()], writes=[ybt.k()], dur=dve_d(128))
                py = pb.next()
                P.op("PE", TR([(py.bf[:, h * 128:(h + 1) * 128], ybt[:, h * 128:(h + 1) * 128]) for h in range(4)]),
                     reads=[ybt.k(), ident_bf.k()], writes=[py.k()], dur=4 * pe_d(128))
                P.op("ACT", A(ybT[:, :, cs], py.bf[:, 0:512].rearrange("p (h c) -> p h c", h=4), AF.Copy),
                     reads=[py.k()], writes=[ybT.k(s)], dur=act_d(512))
            ds_prev = dsv

            P.tag = f"t{t}:SC"
            def sc1(s):
                cs = slice(s * SUB, (s + 1) * SUB)
                r0 = r_t + s * SUB
                xr = xr_ring.next()
                P.dma(xr[:, :], x_d[r0:r0 + SUB, :], writes=[xr.k()], nbytes=128 * D * 4)
                yk = [yaT.k(j) for j in range(4)] + [ybT.k(s)]
                for half in range(2):
                    hs_ = slice(half * 512, (half + 1) * 512)
                    ph = pc.next()
                    items = []
                    for cc in range(8):
                        lhsT = yaT[:, cc, cs] if cc < 4 else ybT[:, cc - 4, cs]
                        items.append((ph[:, :], lhsT, w_out_bf[:, cc, hs_], cc == 0, cc == 7))
                    P.op("PE", MM(items), reads=yk + [("w_out", kc) for kc in range(8)], writes=[ph.k()], dur=8 * pe_d(512))
                    P.op("DVE", TTo(xr[:, hs_], ph[:, :], xr[:, hs_], ALU.add), reads=[ph.k(), xr.k()], writes=[xr.k()],
                         dur=dve_d(512))
                hn = hn_ring.next()
                st = st_sc.next()
                P.op("ACT", A(hn[:, :], xr[:, :], AF.Square, accum=st[:, 0:1]), reads=[xr.k()], writes=[hn.k(), st.k()],
                     dur=act_d(1024))
                rstd_ops(st, 0, 1, 2, 1, 1.0 / D)
                P.op("POOL", TS(hn[:, :], xr[:, :], st[:, 2:3], 0.0, ALU.mult, ALU.add),
                     reads=[xr.k(), st.k()], writes=[hn.k()], dur=pts_d(1024))
                return xr, hn

            def sc2(s, xr, hn):
                cs = slice(s * SUB, (s + 1) * SUB)
                r0 = r_t + s * SUB
                pt = pc.next()
                P.op("PE", TR([(pt.bf[:, kc * 128:(kc + 1) * 128], hn[:, kc * 128:(kc + 1) * 128]) for kc in range(8)]),
                     reads=[hn.k(), ident_bf.k()], writes=[pt.k()], dur=8 * pe_d(128))
                hnT = hnT_ring.next()
                P.op("DVE", CP(hnT[:, :, :], pt.bf[:, :].rearrange("p (k n) -> p k n", k=8)),
                     reads=[pt.k()], writes=[hnT.k()], dur=dve_d(1024, 2))
                psub = p_ring.next()
                P.dma(psub[:, :], p_d[r0:r0 + SUB, :], writes=[psub.k()], nbytes=128 * PLE * 4)
                pbf = pbf_ring.next()
                P.op("POOL", CP(pbf[:, :], psub[:, :]), reads=[psub.k()], writes=[pbf.k()], dur=pts_d(256))
                ppt = pc.next()
                P.op("PE", TR([(ppt.bf[:, kc * 128:(kc + 1) * 128], pbf[:, kc * 128:(kc + 1) * 128]) for kc in range(2)]),
                     reads=[pbf.k(), ident_bf.k()], writes=[ppt.k()], dur=2 * pe_d(128))
                pT = pT_ring.next()
                P.op("ACT", A(pT[:, :, :], ppt.bf[:, 0:256].rearrange("p (k n) -> p k n", k=2), AF.Copy),
                     reads=[ppt.k()], writes=[pT.k()], dur=act_d(256))
                tg = tg_ring.next()
                for half in range(2):
                    hs_ = slice(half * 512, (half + 1) * 512)
                    pg = pc.next()
                    items = [(pg[:, :], hnT[:, kc, :], w_gate_bf[:, kc, hs_], kc == 0, False) for kc in range(8)]
                    items.append((pg[:, :], ones_bf[0:1, :], bg_bf[0:1, hs_], False, True))
                    P.op("PE", MM(items), reads=[hnT.k(), ones_bf.k(), bg_bf.k()] + [("w_gate", kc) for kc in range(8)],
                         writes=[pg.k()], dur=9 * pe_d(512))
                    P.op("ACT", A(tg[:, hs_], pg[:, :], AF.Tanh, scale=0.5), reads=[pg.k()], writes=[tg.k()],
                         dur=act_d(512), aset="silu")
                    ppp = pc.next()
                    P.op("PE", MM([(ppp[:, :], pT[:, kc, :], w_pp_bf[:, kc, hs_], kc == 0, kc == 1) for kc in range(2)]),
                         reads=[pT.k(), ("w_pp", 0), ("w_pp", 1)], writes=[ppp.k()], dur=2 * pe_d(512))
                    P.op("DVE", STT(tg[:, hs_], tg[:, hs_], 1.0, ppp[:, :], ALU.add, ALU.mult),
                         reads=[tg.k(), ppp.k()], writes=[tg.k()], dur=dve_d(512))
                P.op("DVE", STT(tg[:, :], tg[:, :], 0.5, xr[:, :], ALU.mult, ALU.add),
                     reads=[tg.k(), xr.k()], writes=[tg.k()], dur=dve_d(1024))
                st3 = st_sc.next()
                P.op("ACT", A(hnT[:, :, :].rearrange("p k n -> p (k n)"), tg[:, :], AF.Square, accum=st3[:, 0:1]), reads=[tg.k()], writes=[hnT.k(), st3.k()],
                     dur=act_d(1024))
                rstd_ops(st3, 0, 1, 2, 1, 1.0 / D)
                P.op("DVE", STT(tg[:, :], tg[:, :], st3[:, 2:3], fnw[:, :], ALU.mult, ALU.mult),
                     reads=[tg.k(), st3.k(), fnw.k()], writes=[tg.k()], dur=dve_d(1024))
                ok = ("out", r0)
                P.dma(out_d[r0:r0 + SUB, :], tg[:, :], reads=[tg.k()], writes=[ok], nbytes=128 * D * 4)
                out_keys.append(ok)

            pend = {}
            pend[0] = sc1(0)
            for s in range(1, 4):
                pend[s] = sc1(s)
                sc2(s - 1, *pend.pop(s - 1))
            sc2(3, *pend.pop(3))

        P.op("SP", lambda e: None, reads=out_keys, name="final")
        P.schedule()
        P.emit(block, sems, dsems)
        build_program.info = dict(n_switch=getattr(P, 'n_switch', 0), sbuf_bytes=sbytes[0], est_us=P.est_time, n_ops=len(P.ops), n_waits=P.n_waits)
    return nc


def host_layout(x, p, norm_mix_w, w_in, conv_w, conv_b, rg_wa, rg_ba, rg_wx, rg_bx,
                rg_lambda, hg_lb, hg_norm_w, w_out, ple_norm_w, w_ple_gate, b_ple_gate,
                w_ple_proj, final_norm_w):
    f = np.float32
    vecs = np.zeros((128, NV), f)
    vecs[:, V_NMW:V_NMW + 8] = np.asarray(norm_mix_w[0], f).reshape(8, 128).T
    vecs[:, V_PNW:V_PNW + 8] = np.asarray(ple_norm_w[0], f).reshape(8, 128).T
    cw = np.asarray(conv_w[0], f).reshape(4, 4, 128)
    for tap in range(4):
        vecs[:, V_CW + tap * 4:V_CW + tap * 4 + 4] = cw[tap].T
    vecs[:, V_CB:V_CB + 4] = np.asarray(conv_b[0], f).reshape(4, 128).T
    vecs[:, V_BA:V_BA + 4] = np.asarray(rg_ba[0], f).reshape(4, 128).T
    vecs[:, V_BX:V_BX + 4] = np.asarray(rg_bx[0], f).reshape(4, 128).T
    vecs[:, V_LAM:V_LAM + 4] = np.asarray(rg_lambda[0], f).reshape(4, 128).T
    vecs[:, V_LB0:V_LB0 + 4] = np.asarray(hg_lb[0], f).reshape(4, 128).T
    vecs[:, V_LB1:V_LB1 + 4] = np.asarray(hg_lb[1], f).reshape(4, 128).T
    vecs[:, V_NW] = np.asarray(hg_norm_w[0], f)

    def bd(w):
        w = np.asarray(w[0], f)
        o = np.zeros((128, 4, 128), f)
        for g in range(8):
            j, q = g // 2, g % 2
            o[q * 64:(q + 1) * 64, j, q * 64:(q + 1) * 64] = w[g]
        return np.ascontiguousarray(o.reshape(128, 512))
    shared = {
        "w_in": np.ascontiguousarray(np.asarray(w_in[0], f)),
        "w_out": np.ascontiguousarray(np.asarray(w_out[0], f)),
        "w_gate": np.ascontiguousarray(np.asarray(w_ple_gate[0], f)),
        "w_pp": np.ascontiguousarray(np.asarray(w_ple_proj[0], f)),
        "vecs": vecs,
        "fnw": np.ascontiguousarray(np.broadcast_to(np.asarray(final_norm_w, f)[None, :], (128, D))),
        "bg": np.ascontiguousarray(np.asarray(b_ple_gate[0], f)[None, :]),
        "wabd": bd(rg_wa),
        "wxbd": bd(rg_wx),
        "ident": np.eye(128, dtype=f),
        "mask": np.triu(np.ones((128, 128), f)),
    }
    return shared


_NC_CACHE = {}


def kernel(**inputs):
    x = np.asarray(inputs["x"], np.float32)
    p = np.asarray(inputs["p"], np.float32)
    B, T, _ = x.shape
    shared = host_layout(**inputs)
    if T not in _NC_CACHE:
        _NC_CACHE[T] = build_program(T)
    nc = _NC_CACHE[T]
    in_maps = []
    for b in range(B):
        m = dict(shared)
        m["x"] = np.ascontiguousarray(x[b])
        m["p"] = np.ascontiguousarray(p[0, b])
        in_maps.append(m)
    res = run_bass_kernel_spmd(nc, in_maps, core_ids=list(range(B)))
    return np.stack([np.asarray(r["out"], np.float32) for r in res.results], axis=0)
```

```python
import contextlib
import heapq
import numpy as np
import concourse.bass as bass
import concourse.mybir as mybir
from concourse.bass_utils import run_bass_kernel_spmd

F32 = mybir.dt.float32
BF16 = mybir.dt.bfloat16
AF = mybir.ActivationFunctionType
ALU = mybir.AluOpType
AX = mybir.AxisListType

ENGS = ("PE", "ACT", "DVE", "POOL", "SP")
ACT_WAIT = 0.25
USE_BL = False
USE_BL_PE = True
BL_LAT = 0.9
BL_ENGS = ("PE", "DVE", "POOL", "ACT")
EVAC_FIRST = True
PESSIMISM = 1.35
SWITCH_COST = 2.0
HOLD = 0.0
MIN_BATCH = 1


class Op:
    __slots__ = ("id", "eng", "fn", "dur", "aset", "deps", "name", "pos", "sig",
                 "sigval", "is_dma", "dsem", "dval", "lat", "users", "t_end", "t_start", "pri")

    def __init__(self, id, eng, fn, dur, aset, name, is_dma=False, lat=0.0):
        self.id = id
        self.eng = eng
        self.fn = fn
        self.dur = dur
        self.aset = aset
        self.deps = {}
        self.name = name
        self.pos = -1
        self.sig = False
        self.sigval = 0
        self.is_dma = is_dma
        self.dsem = None
        self.dval = 0
        self.lat = lat
        self.users = []
        self.t_end = 0.0
        self.pri = 1


class Prog:
    N_DMA_SEMS = 24

    def __init__(self, nc):
        self.nc = nc
        self.ops = []
        self.last_w = {}
        self.readers = {}
        self.ndma = 0
        self.dma_ops = []
        self.tag = ""

    def _add(self, op, reads, writes):
        for k in reads:
            for d in self.last_w.get(k, ()):
                op.deps[d] = True
        for k in writes:
            for d in self.last_w.get(k, ()):
                op.deps.setdefault(d, False)
            for d in self.readers.get(k, ()):
                if d != op.id:
                    op.deps.setdefault(d, False)
        for k in reads:
            self.readers.setdefault(k, []).append(op.id)
        for k in writes:
            self.last_w[k] = [op.id]
            self.readers[k] = []
        op.deps.pop(op.id, None)
        self.ops.append(op)
        return op

    def op(self, eng, fn, reads=(), writes=(), dur=0.3, aset=None, name=""):
        o = Op(len(self.ops), eng, fn, dur, aset, name or self.tag)
        if EVAC_FIRST and eng in ("ACT", "DVE") and any(isinstance(k, tuple) and str(k[0]).startswith("p") and "_" in str(k[0])
                                                      and str(k[0]).split("_")[0] in ("pa", "pb", "pc", "pn") for k in reads):
            o.pri = 0
        return self._add(o, reads, writes)

    def dma(self, out, in_, reads=(), writes=(), nbytes=0, name="dma"):
        def fn(e, out=out, in_=in_):
            return e.dma_start(out=out, in_=in_)
        o = Op(len(self.ops), "SP", fn, 0.08, None, self.tag, is_dma=True,
               lat=2.0 + nbytes / 250e3)
        self.ndma += 1
        self.dma_ops.append(o)
        return self._add(o, reads, writes)

    def schedule(self):
        ops = self.ops
        n = len(ops)
        for o in ops:
            o.users = []
        indeg = [0] * n
        for o in ops:
            for d in o.deps:
                ops[d].users.append(o.id)
            indeg[o.id] = len(o.deps)
        bl = [0.0] * n
        for o in reversed(ops):
            m = 0.0
            for u in o.users:
                c = bl[u] + (BL_LAT if ops[u].eng != o.eng else 0.0) + (o.lat if o.is_dma else 0.0)
                if c > m:
                    m = c
            bl[o.id] = o.dur * (1.0 if o.eng in ("PE", "SP") else PESSIMISM) + m
        self.bl = bl
        ready = {e: [] for e in ENGS}
        ready_t = [0.0] * n
        for o in ops:
            if indeg[o.id] == 0:
                heapq.heappush(ready[o.eng], o.id)
        free_t = {e: 0.0 for e in ENGS}
        order = {e: [] for e in ENGS}
        cur_set = None
        done = 0
        slot_last = [None] * self.N_DMA_SEMS
        slot_end = [0.0] * self.N_DMA_SEMS
        slot_cnt = [0] * self.N_DMA_SEMS
        SEM_LAT = 0.85
        while done < n:
            best = None
            for e in ENGS:
                if not ready[e]:
                    continue
                cands = heapq.nsmallest(16, ready[e])
                ft = free_t[e]
                pick = None
                pick_key = None
                n_other = 0
                if e == "ACT" and cur_set is not None:
                    for cid in cands:
                        if ops[cid].aset is not None and ops[cid].aset != cur_set and ready_t[cid] <= ft + 0.25:
                            n_other += 1
                for cid in cands:
                    st = max(ft, ready_t[cid])
                    sw = (e == "ACT" and ops[cid].aset is not None and cur_set is not None
                          and ops[cid].aset != cur_set)
                    if sw and n_other < MIN_BATCH:
                        st = max(st, ready_t[cid] + HOLD)
                    late = st > ft + (0.25 if (sw or e != "ACT") else ACT_WAIT)
                    pr = (ops[cid].pri, -bl[cid] if (USE_BL or (USE_BL_PE and e in BL_ENGS)) else cid)
                    if not late:
                        key = (0, 1 if sw else 0, pr, st)
                    else:
                        key = (1, st + (SWITCH_COST if sw else 0.0), pr, st)
                    if pick_key is None or key < pick_key:
                        pick_key = key
                        pick = cid
                pick_key = (pick_key[3],)
                st = pick_key[0]
                if best is None or (st, pick) < (best[0], best[2]):
                    best = (st, e, pick)
            st, e, cid = best
            o = ops[cid]
            ready[e].remove(cid)
            heapq.heapify(ready[e])
            if e == "ACT" and o.aset is not None:
                if cur_set is not None and o.aset != cur_set:
                    st += SWITCH_COST
                    self.n_switch = getattr(self, "n_switch", 0) + 1
                cur_set = o.aset
            if o.is_dma:
                sl = min(range(self.N_DMA_SEMS), key=lambda i: slot_end[i])
                if slot_last[sl] is not None:
                    o.deps.setdefault(slot_last[sl], False)
                    st = max(st, slot_end[sl] + SEM_LAT)
                slot_cnt[sl] += 1
                o.dsem = sl
                o.dval = 16 * slot_cnt[sl]
                slot_last[sl] = cid
                slot_end[sl] = st + o.dur + o.lat
            o.pos = len(order[e])
            order[e].append(cid)
            end_issue = st + o.dur * (1.0 if e in ("PE", "SP") else PESSIMISM)
            free_t[e] = end_issue
            o.t_end = end_issue + (o.lat if o.is_dma else 0.0)
            o.t_start = st
            done += 1
            for u in o.users:
                indeg[u] -= 1
                rt = o.t_end + (SEM_LAT if (ops[u].eng != e or o.is_dma) else 0.0)
                if rt > ready_t[u]:
                    ready_t[u] = rt
                if indeg[u] == 0:
                    heapq.heappush(ready[ops[u].eng], u)
        self.order = order
        self.est_time = max(o.t_end for o in ops)
        return order

    def emit(self, block, sems, dma_sems):
        ops = self.ops
        order = self.order
        SAME_ENG_DIST = 3

        def needs_wait(o, d, is_raw):
            dop = ops[d]
            if dop.is_dma:
                return True
            if dop.eng != o.eng:
                return True
            if o.eng in ("PE", "SP"):
                return False
            return True

        for o in ops:
            for d, is_raw in o.deps.items():
                if needs_wait(o, d, is_raw):
                    ops[d].sig = True
        for e in ENGS:
            c = 0
            for cid in order[e]:
                o = ops[cid]
                if o.is_dma:
                    continue
                if o.sig:
                    c += 1
                    o.sigval = c
        clock = [None] * len(ops)
        eng_known = {e: {} for e in ENGS}
        waits = {}
        ptr = {e: 0 for e in ENGS}
        remaining = len(ops)

        def semkey_of(dop):
            return ("d", dop.dsem) if dop.is_dma else ("e", dop.eng)

        def val_of(dop):
            return dop.dval if dop.is_dma else dop.sigval

        while remaining:
            progressed = False
            for e in ENGS:
                while ptr[e] < len(order[e]):
                    o = ops[order[e][ptr[e]]]
                    if any(clock[d] is None for d in o.deps):
                        break
                    known = eng_known[e]
                    wl = []
                    need = [d for d, r in o.deps.items() if needs_wait(o, d, r)]
                    need.sort(key=lambda d: -val_of(ops[d]))
                    for d in need:
                        dop = ops[d]
                        sk = semkey_of(dop)
                        v = val_of(dop)
                        if known.get(sk, 0) >= v:
                            continue
                        wl.append((sk, v))
                        for k2, v2 in clock[d].items():
                            if known.get(k2, 0) < v2:
                                known[k2] = v2
                    waits[o.id] = wl
                    ck = dict(known)
                    if o.is_dma:
                        ck[("d", o.dsem)] = o.dval
                    elif o.sig:
                        ck[("e", e)] = o.sigval
                        known[("e", e)] = max(known.get(("e", e), 0), 0)
                    clock[o.id] = ck
                    ptr[e] += 1
                    remaining -= 1
                    progressed = True
            assert progressed, "scheduler deadlock"

        def semh(sk):
            return dma_sems[sk[1]] if sk[0] == "d" else sems[sk[1]]

        self.n_waits = sum(len(w) for w in waits.values())

        def run_engine(e):
            def body(eng):
                for cid in order[e]:
                    o = ops[cid]
                    for sk, v in waits[cid]:
                        eng.wait_ge(semh(sk), v)
                    ins = o.fn(eng)
                    if ins is None:
                        continue
                    if o.is_dma:
                        ins.then_inc(dma_sems[o.dsem], 16)
                    elif o.sig:
                        ins.then_inc(sems[e], 1)
            return body

        block.tensor(run_engine("PE"))
        block.scalar(run_engine("ACT"))
        block.vector(run_engine("DVE"))
        block.gpsimd(run_engine("POOL"))
        block.sync(run_engine("SP"))


class Buf:
    def __init__(self, t, name):
        self.t = t
        self.name = name

    def __getitem__(self, idx):
        return self.t[idx]

    def k(self, part=None):
        return (self.name, part)


class Ring:
    def __init__(self, bufs):
        self.bufs = bufs
        self.i = 0

    def next(self):
        b = self.bufs[self.i % len(self.bufs)]
        self.i += 1
        return b


D = 1024
DIN = 3072
PLE = 256
TT = 512
SUB = 128
NV = 64
EPS = 1e-6
HS = 128 ** -0.5

V_NMW, V_PNW, V_CW, V_CB, V_BA, V_BX, V_LAM, V_LB0, V_LB1, V_NW = 0, 8, 16, 32, 36, 40, 44, 48, 52, 56
C_HCL, C_CL, C_HBA, C_HBX, C_FS, C_FB, C_KN, C_KP, C_E, C_Y, C_Y2, C_PL, C_TH, C_M05, C_ZERO = \
    0, 4, 8, 12, 16, 20, 24, 28, 32, 36, 40, 44, 48, 52, 53


def act_d(n):
    return 0.16 + n / 1200.0


def dve_d(n, mode=1.0):
    return (n / mode + 151) / 960.0


def pts_d(n):
    return 0.12 + n / 1050.0


def ptt_d(n):
    return 0.15 + n * 2.25 / 1000.0


def pe_d(n):
    return 0.045 + max(n, 64) / 2400.0


def A(out, in_, func, bias=0.0, scale=1.0, accum=None):
    def f(e):
        if accum is not None:
            return e.activation(out=out, in_=in_, func=func, bias=bias, scale=scale, accum_out=accum)
        return e.activation(out=out, in_=in_, func=func, bias=bias, scale=scale)
    return f


def TS(out, in0, s1, s2, op0, op1):
    return lambda e: e.tensor_scalar(out=out, in0=in0, scalar1=s1, scalar2=s2, op0=op0, op1=op1)


def TTo(out, in0, in1, op):
    return lambda e: e.tensor_tensor(out=out, in0=in0, in1=in1, op=op)


def STT(out, in0, scalar, in1, op0, op1):
    return lambda e: e.scalar_tensor_tensor(out=out, in0=in0, scalar=scalar, in1=in1, op0=op0, op1=op1)


def CP(out, in_):
    return lambda e: e.tensor_copy(out=out, in_=in_)


def MS(out, v):
    return lambda e: e.memset(out, v)


def MM(items):
    def f(e):
        r = None
        for (out, lhsT, rhs, st, sp) in items:
            r = e.matmul(out, lhsT=lhsT, rhs=rhs, start=st, stop=sp)
        return r
    return f


def build_program(T):
    NT = T // TT
    nc = bass.Bass("TRN2", target_bir_lowering=False)

    def din(name, shape):
        return nc.dram_tensor(name, shape, F32, kind="ExternalInput").ap()

    x_d = din("x", [T, D])
    p_d = din("p", [T, PLE])
    w_in_d = din("w_in", [D, DIN])
    w_out_d = din("w_out", [D, D])
    w_gate_d = din("w_gate", [D, D])
    w_pp_d = din("w_pp", [PLE, D])
    vecs_d = din("vecs", [128, NV])
    fnw_d = din("fnw", [128, D])
    bg_d = din("bg", [1, D])
    wabd_d = din("wabd", [128, 512])
    wxbd_d = din("wxbd", [128, 512])
    ident_d = din("ident", [128, 128])
    mask_d = din("mask", [128, 128])
    out_d = nc.dram_tensor("out", [T, D], F32, kind="ExternalOutput").ap()

    with contextlib.ExitStack() as es:
        cnt = [0]

        sbytes = [0]

        def sb(name, shape, dt=F32):
            cnt[0] += 1
            nm = f"{name}_{cnt[0]}"
            sbytes[0] += int(np.prod(shape[1:])) * (2 if dt == BF16 else 4)
            return Buf(es.enter_context(nc.sbuf_tensor(nm, shape, dt)), nm)

        def psb(name):
            cnt[0] += 1
            nm = f"{name}_{cnt[0]}"
            b = Buf(es.enter_context(nc.psum_tensor(nm, [128, 512], F32)), nm)
            b.bf = b.t[:, :].bitcast(BF16)
            return b

        def ring(name, n, shape, dt=F32):
            return Ring([sb(name, shape, dt) for _ in range(n)])

        w_in_bf = sb("w_in_bf", [128, 8, DIN], BF16)
        w_out_bf = sb("w_out_bf", [128, 8, D], BF16)
        w_gate_bf = sb("w_gate_bf", [128, 8, D], BF16)
        w_pp_bf = sb("w_pp_bf", [128, 2, D], BF16)
        vecs = sb("vecs", [128, NV])
        dv = sb("dv", [128, 64])
        ident_bf = sb("ident_bf", [128, 128], BF16)
        mask_f = sb("mask_f", [128, 128])
        fnw = sb("fnw", [128, D])
        bg_bf = sb("bg_bf", [1, D], BF16)
        ones_bf = sb("ones_bf", [1, 128], BF16)
        wabd_bf = sb("wabd_bf", [128, 512], BF16)
        wxbd_bf = sb("wxbd_bf", [128, 512], BF16)
        halo = sb("halo", [128, 4, 3])
        hlast = sb("hlast", [128, 4])
        Z = sb("Z", [128, 4, 128])
        xs_ring = ring("xs", 2, [128, D])
        xr_ring = ring("xr", 2, [128, D])
        tg_ring = ring("tg", 1, [128, D])
        xn_ring = ring("xn", 2, [128, D], BF16)
        hn_ring = ring("hn", 2, [128, D], BF16)
        uT_ring = ring("uT", 2, [128, 8, TT], BF16)
        xa_ring = ring("xa", 2, [128, TT + 3])
        xcb_ring = ring("xcb", 2, [128, TT], BF16)
        c2k = ring("c2k", 10, [128, TT])
        c1k = ring("c1k", 3, [128, TT], BF16)
        yaT_ring = ring("yaT", 2, [128, 4, TT], BF16)
        ybT_ring = ring("ybT", 2, [128, 4, TT], BF16)
        v_ring = ring("v_bf", 2, [128, TT], BF16)
        gs_ring = ring("gs_bf", 2, [128, TT], BF16)
        qeT = sb("qeT", [128, 4, TT], BF16)
        keT = sb("keT", [128, 4, TT], BF16)
        ket_ring = ring("ket", 1, [128, TT], BF16)
        scb_ring = ring("scb", 1, [128, TT], BF16)
        ybt_ring = ring("ybt", 1, [128, TT], BF16)
        W_ring = ring("W", 2, [128, 4, 128], BF16)
        ds_ring = ring("dsv", 2, [128, 4, 4])
        st_n1 = ring("stn1", 4, [128, 12])
        st_gla = ring("stgla", 2, [128, 12])
        st_sc = ring("stsc", 4, [128, 12])
        hnT_ring = ring("hnT", 1, [128, 8, 128], BF16)
        p_ring = ring("psub", 2, [128, PLE])
        pbf_ring = ring("pbf", 1, [128, PLE], BF16)
        pT_ring = ring("pT", 1, [128, 2, 128], BF16)
        pn = Ring([psb("pn") for _ in range(1)])
        pa = Ring([psb("pa") for _ in range(2)])
        pb = Ring([psb("pb") for _ in range(3)])
        pc = Ring([psb("pc") for _ in range(2)])

        sems = {e: es.enter_context(nc.semaphore(f"s_{e}")) for e in ENGS}
        dsems = [es.enter_context(nc.semaphore(f"d_{i}")) for i in range(Prog.N_DMA_SEMS)]
        block = es.enter_context(nc.Block())
        P = Prog(nc)

        def vcol(c):
            return vecs[:, c:c + 1]

        def dcol(c):
            return dv[:, c:c + 1]

        def TR(items):
            def f(e):
                r = None
                for (out, in_) in items:
                    r = e.transpose(out=out, in_=in_, identity=ident_bf[:, :])
                return r
            return f

        P.dma(vecs[:, :], vecs_d, writes=[vecs.k()], nbytes=128 * NV * 4)
        P.dma(mask_f[:, :], mask_d, writes=[mask_f.k()], nbytes=65536)
        P.dma(fnw[:, :], fnw_d, writes=[fnw.k()], nbytes=128 * D * 4)
        P.op("POOL", MS(halo[:, :, :], 0.0), writes=[halo.k(j) for j in range(4)])
        P.op("POOL", MS(hlast[:, :], 0.0), writes=[hlast.k(j) for j in range(4)])
        P.op("POOL", MS(Z[:, :, :], 0.0), writes=[Z.k()])
        P.op("POOL", MS(ones_bf[:, :], 1.0), writes=[ones_bf.k()])
        W0 = W_ring.next()
        P.op("POOL", MS(W0[:, :, :], 0.0), writes=[W0.k()])
        ds_prev = ds_ring.next()
        P.op("POOL", MS(ds_prev[:, :, :], 0.0), writes=[ds_prev.k(h) for h in range(4)])
        pre_x = []
        for s in range(2):
            xb_ = xs_ring.next()
            P.dma(xb_[:, :], x_d[s * SUB:(s + 1) * SUB, :], writes=[xb_.k()], nbytes=128 * D * 4)
            pre_x.append(xb_)

        dk = [dv.k()]
        vk = [vecs.k()]

        def dts(out_c, n, in_ap, s1, s2, op0=ALU.mult, op1=ALU.add, eng="DVE"):
            P.op(eng, TS(dv[:, out_c:out_c + n], in_ap, s1, s2, op0, op1), reads=dk + vk, writes=dk, dur=0.2)

        def dtt(out_c, n, a_ap, b_ap, op, eng="DVE"):
            P.op(eng, TTo(dv[:, out_c:out_c + n], a_ap, b_ap, op), reads=dk + vk, writes=dk, dur=0.2)

        P.op("POOL", MS(dv[:, :], 0.0), writes=dk)
        P.op("POOL", MS(dv[:, C_M05:C_M05 + 1], -0.5), reads=dk, writes=dk)
        dts(C_HBA, 4, vecs[:, V_BA:V_BA + 4], 0.5, 0.0)
        dts(C_HBX, 4, vecs[:, V_BX:V_BX + 4], 0.5, 0.0)
        P.op("ACT", A(dv[:, C_E:C_E + 4], vecs[:, V_LAM:V_LAM + 4], AF.Exp, scale=-1.0),
             reads=dk + vk, writes=dk, dur=0.4, aset="lnexp")
        dts(C_Y, 4, dv[:, C_E:C_E + 4], 1.0, 2.0)
        P.op("DVE", lambda e: e.reciprocal(out=dv[:, C_Y:C_Y + 4], in_=dv[:, C_Y:C_Y + 4]), reads=dk, writes=dk, dur=0.2)
        dtt(C_Y, 4, dv[:, C_Y:C_Y + 4], dv[:, C_E:C_E + 4], ALU.mult)
        dtt(C_Y2, 4, dv[:, C_Y:C_Y + 4], dv[:, C_Y:C_Y + 4], ALU.mult)
        dts(C_PL, 4, dv[:, C_Y2:C_Y2 + 4], 1.0 / 11.0, 1.0 / 9.0)
        for cst in (1.0 / 7.0, 1.0 / 5.0, 1.0 / 3.0, 1.0):
            dtt(C_PL, 4, dv[:, C_PL:C_PL + 4], dv[:, C_Y2:C_Y2 + 4], ALU.mult)
            dts(C_PL, 4, dv[:, C_PL:C_PL + 4], 1.0, cst)
        dtt(C_PL, 4, dv[:, C_PL:C_PL + 4], dv[:, C_Y:C_Y + 4], ALU.mult)
        dts(C_CL, 4, dv[:, C_PL:C_PL + 4], -16.0, 0.0)
        dts(C_HCL, 4, dv[:, C_PL:C_PL + 4], -8.0, 0.0)
        dtt(C_TH, 4, vecs[:, V_LB0:V_LB0 + 4], vecs[:, V_LB1:V_LB1 + 4], ALU.subtract)
        P.op("ACT", A(dv[:, C_TH:C_TH + 4], dv[:, C_TH:C_TH + 4], AF.Tanh, scale=0.5),
             reads=dk, writes=dk, dur=0.4, aset="silu")
        dts(C_FS, 4, dv[:, C_TH:C_TH + 4], -0.25, 0.25)
        dts(C_FB, 4, dv[:, C_TH:C_TH + 4], 0.25, 0.75)
        dts(C_KP, 4, dv[:, C_TH:C_TH + 4], -0.25 * HS, 0.25 * HS)
        dts(C_KN, 4, dv[:, C_TH:C_TH + 4], 0.25 * HS, -0.25 * HS)

        stage_bufs = xr_ring.bufs + tg_ring.bufs
        sidx = [0]

        def stage():
            b = stage_bufs[sidx[0] % len(stage_bufs)]
            sidx[0] += 1
            return b

        sgb = stage()
        P.dma(sgb[:, 0:128], ident_d, writes=[sgb.k()], nbytes=65536)
        P.op("DVE", CP(ident_bf[:, :], sgb[:, 0:128]), reads=[sgb.k()], writes=[ident_bf.k()], dur=0.3)
        sgb = stage()
        P.dma(sgb[:, 0:512], wabd_d, writes=[sgb.k()], nbytes=128 * 512 * 4)
        P.op("DVE", CP(wabd_bf[:, :], sgb[:, 0:512]), reads=[sgb.k()], writes=[wabd_bf.k()], dur=dve_d(512, 2))
        sgb = stage()
        P.dma(sgb[:, 0:512], wxbd_d, writes=[sgb.k()], nbytes=128 * 512 * 4)
        P.op("DVE", CP(wxbd_bf[:, :], sgb[:, 0:512]), reads=[sgb.k()], writes=[wxbd_bf.k()], dur=dve_d(512, 2))
        sgb = stage()
        P.dma(sgb[0:1, :], bg_d, writes=[sgb.k()], nbytes=4096)
        P.op("DVE", CP(bg_bf[:, :], sgb[0:1, :]), reads=[sgb.k()], writes=[bg_bf.k()], dur=dve_d(1024, 2))
        w_in_v = w_in_d.rearrange("(kc p) n -> p kc n", p=128)
        for ci, cc in enumerate([0, 4, 1, 5, 2, 6, 3, 7, 12, 8, 13, 9, 14, 10, 15, 11] + list(range(16, 24))):
            sgb = stage()
            sv = sgb[:, :].rearrange("p (k n) -> p k n", k=8)
            P.dma(sv, w_in_v[:, :, cc * 128:(cc + 1) * 128], writes=[sgb.k()], nbytes=128 * 1024 * 4)
            eng = "DVE" if ci % 2 == 0 else "POOL"
            P.op(eng, TTo(w_in_bf[:, :, cc * 128:(cc + 1) * 128], sv,
                          vecs[:, V_NMW:V_NMW + 8].unsqueeze(2).broadcast_to([128, 8, 128]), ALU.mult),
                 reads=[sgb.k(), vecs.k()], writes=[("w_in", cc)],
                 dur=dve_d(1024) if eng == "DVE" else ptt_d(1024))
        def late_weights():
            for kc in range(8):
                sgb = stage()
                P.dma(sgb[:, :], w_out_d[kc * 128:(kc + 1) * 128, :], writes=[sgb.k()], nbytes=128 * 1024 * 4)
                if kc < 4:
                    P.op("DVE", CP(w_out_bf[:, kc, :], sgb[:, :]), reads=[sgb.k()], writes=[("w_out", kc)], dur=dve_d(1024, 2))
                else:
                    P.op("ACT", A(w_out_bf[:, kc, :], sgb[:, :], AF.Identity, scale=vcol(V_NW)),
                         reads=[sgb.k(), vecs.k()], writes=[("w_out", kc)], dur=act_d(1024))
            for kc in range(8):
                sgb = stage()
                P.dma(sgb[:, :], w_gate_d[kc * 128:(kc + 1) * 128, :], writes=[sgb.k()], nbytes=128 * 1024 * 4)
                P.op("ACT", A(w_gate_bf[:, kc, :], sgb[:, :], AF.Identity, scale=vcol(V_PNW + kc)),
                     reads=[sgb.k(), vecs.k()], writes=[("w_gate", kc)], dur=act_d(1024))
            for kc in range(2):
                sgb = stage()
                P.dma(sgb[:, :], w_pp_d[kc * 128:(kc + 1) * 128, :], writes=[sgb.k()], nbytes=128 * 1024 * 4)
                P.op("DVE", CP(w_pp_bf[:, kc, :], sgb[:, :]), reads=[sgb.k()], writes=[("w_pp", kc)], dur=dve_d(1024, 2))

        W_in_all = [("w_in", cc) for cc in range(24)]

        def rstd_ops(st, c_in, c_tmp, c_out, n, inv_n):
            P.op("POOL", TS(st[:, c_tmp:c_tmp + n], st[:, c_in:c_in + n], inv_n, EPS, ALU.mult, ALU.add),
                 reads=[st.k()], writes=[st.k()], dur=0.35)
            P.op("POOL", TTo(st[:, c_out:c_out + n], st[:, c_tmp:c_tmp + n],
                             dv[:, C_M05:C_M05 + 1].broadcast_to([128, n]), ALU.pow),
                 reads=[st.k(), dv.k()], writes=[st.k()], dur=0.75)

        W_cur = W0
        out_keys = []
        for t in range(NT):
            r_t = t * TT
            uT = uT_ring.next()
            yaT = yaT_ring.next()
            ybT = ybT_ring.next()
            P.tag = f"t{t}:N1"
            for s in range(4):
                if t == 0 and s < 2:
                    xs = pre_x[s]
                else:
                    xs = xs_ring.next()
                    P.dma(xs[:, :], x_d[r_t + s * SUB:r_t + (s + 1) * SUB, :], writes=[xs.k()], nbytes=128 * D * 4)
                xn = xn_ring.next()
                st = st_n1.next()
                P.op("ACT", A(xn[:, :], xs[:, :], AF.Square, accum=st[:, 0:1]),
                     reads=[xs.k()], writes=[xn.k(), st.k()], dur=act_d(1024))
                rstd_ops(st, 0, 1, 2, 1, 1.0 / D)
                P.op("POOL", TS(xn[:, :], xs[:, :], st[:, 2:3], 0.0, ALU.mult, ALU.add),
                     reads=[xs.k(), st.k()], writes=[xn.k()], dur=pts_d(1024))
                bank = pn.next()
                P.op("PE", TR([(bank.bf[:, kc * 128:(kc + 1) * 128], xn[:, kc * 128:(kc + 1) * 128]) for kc in range(8)]),
                     reads=[xn.k(), ident_bf.k()], writes=[bank.k()], dur=8 * pe_d(128))
                P.op("DVE", CP(uT[:, :, s * SUB:(s + 1) * SUB], bank.bf[:, :].rearrange("p (k n) -> p k n", k=8)),
                     reads=[bank.k()], writes=[uT.k(s)], dur=dve_d(1024, 2))
            uT_keys = [uT.k(s) for s in range(4)]

            def inproj_cm(cc):
                bank = pa.next()
                P.op("PE", MM([(bank[:, :], w_in_bf[:, kc, cc * 128:(cc + 1) * 128], uT[:, kc, :], kc == 0, kc == 7)
                               for kc in range(8)]),
                     reads=uT_keys + [("w_in", cc)], writes=[bank.k()], dur=8 * pe_d(512))
                return bank

            P.tag = f"t{t}:GA"
            for j in range(4):
                bank = inproj_cm(j)
                xa = xa_ring.next()
                P.op("POOL", CP(xa[:, 0:3], halo[:, j, :]), reads=[halo.k(j)], writes=[xa.k()], dur=0.15)
                P.op("ACT", A(xa[:, 3:TT + 3], bank[:, :], AF.Copy), reads=[bank.k()], writes=[xa.k()], dur=act_d(512))
                P.op("POOL", CP(halo[:, j, :], xa[:, TT:TT + 3]), reads=[xa.k()], writes=[halo.k(j)], dur=0.15)
                xcf = c2k.next()
                xcb = xcb_ring.next()
                P.op("DVE", TS(xcf[:, :], xa[:, 3:TT + 3], vcol(V_CW + 3 * 4 + j), vcol(V_CB + j), ALU.mult, ALU.add),
                     reads=[xa.k(), vecs.k()], writes=[xcf.k()], dur=dve_d(512, 2))
                for tap in (2, 1, 0):
                    dst = xcb if tap == 0 else xcf
                    P.op("DVE", STT(dst[:, :], xa[:, tap:tap + TT], vcol(V_CW + tap * 4 + j), xcf[:, :], ALU.mult, ALU.add),
                         reads=[xa.k(), vecs.k(), xcf.k()], writes=[dst.k()], dur=dve_d(512))
                thr = c2k.next()
                thi = c2k.next()
                av = c2k.next()
                pr = pn.next()
                P.op("PE", MM([(pr[:, :], wabd_bf[:, j * 128:(j + 1) * 128], xcb[:, :], True, True)]),
                     reads=[xcb.k(), wabd_bf.k()], writes=[pr.k()], dur=pe_d(512))
                P.op("ACT", A(thr[:, :], pr[:, :], AF.Tanh, bias=dcol(C_HBA + j), scale=0.5),
                     reads=[pr.k(), dv.k()], writes=[thr.k()], dur=act_d(512), aset="silu")
                pi = pn.next()
                P.op("PE", MM([(pi[:, :], wxbd_bf[:, j * 128:(j + 1) * 128], xcb[:, :], True, True)]),
                     reads=[xcb.k(), wxbd_bf.k()], writes=[pi.k()], dur=pe_d(512))
                P.op("ACT", A(thi[:, :], pi[:, :], AF.Tanh, bias=dcol(C_HBX + j), scale=0.5),
                     reads=[pi.k(), dv.k()], writes=[thi.k()], dur=act_d(512), aset="silu")
                P.op("ACT", A(av[:, :], thr[:, :], AF.Exp, bias=dcol(C_HCL + j), scale=dcol(C_HCL + j)),
                     reads=[thr.k(), dv.k()], writes=[av.k()], dur=act_d(512), aset="lnexp")
                P.op("DVE", TTo(thr[:, :], av[:, :], av[:, :], ALU.mult),
                     reads=[av.k(), thr.k()], writes=[thr.k()], dur=dve_d(512))
                P.op("ACT", A(thr[:, :], thr[:, :], AF.Ln, bias=1.0, scale=-1.0),
                     reads=[thr.k()], writes=[thr.k()], dur=act_d(512), aset="lnexp")
                P.op("ACT", A(thr[:, :], thr[:, :], AF.Exp, bias=float(np.log(0.5)), scale=0.5),
                     reads=[thr.k()], writes=[thr.k()], dur=act_d(512), aset="lnexp")
                if t == 0:
                    P.op("DVE", MS(thr[:, 0:1], 0.5), reads=[thr.k()], writes=[thr.k()], dur=0.1)
                P.op("DVE", STT(thi[:, :], thi[:, :], 1.0, xcb[:, :], ALU.add, ALU.mult),
                     reads=[thi.k(), xcb.k()], writes=[thi.k()], dur=dve_d(512))
                P.op("POOL", TTo(thi[:, :], thi[:, :], thr[:, :], ALU.mult),
                     reads=[thi.k(), thr.k()], writes=[thi.k()], dur=ptt_d(512))
                hb = c2k.next()
                P.op("DVE", (lambda hb=hb, av=av, thi=thi, j=j: (lambda e: e.tensor_tensor_scan(
                    out=hb[:, :], data0=av[:, :], data1=thi[:, :], initial=hlast[:, j:j + 1],
                    op0=ALU.mult, op1=ALU.add)))(),
                    reads=[av.k(), thi.k(), hlast.k(j)], writes=[hb.k()], dur=dve_d(512))
                P.op("POOL", CP(hlast[:, j:j + 1], hb[:, TT - 1:TT]), reads=[hb.k()], writes=[hlast.k(j)], dur=0.15)
                bank2 = inproj_cm(4 + j)
                sga = c1k.next()
                P.op("ACT", A(sga[:, :], bank2[:, :], AF.Silu), reads=[bank2.k()], writes=[sga.k()],
                     dur=act_d(512), aset="silu")
                P.op("POOL", TTo(yaT[:, j, :], hb[:, :], sga[:, :], ALU.mult),
                     reads=[hb.k(), sga.k()], writes=[yaT.k(j)], dur=ptt_d(512))

            P.tag = f"t{t}:GB"
            dsv = ds_ring.next()
            for h in range(4):
                bank = inproj_cm(12 + h)
                thf = c2k.next()
                P.op("ACT", A(thf[:, :], bank[:, :], AF.Tanh, scale=0.5), reads=[bank.k()], writes=[thf.k()],
                     dur=act_d(512), aset="silu")
                fg = c2k.next()
                P.op("POOL", TS(fg[:, :], thf[:, :], dcol(C_FS + h), dcol(C_FB + h), ALU.mult, ALU.add),
                     reads=[thf.k(), dv.k()], writes=[fg.k()], dur=pts_d(512))
                P.op("POOL", TS(thf[:, :], thf[:, :], dcol(C_KN + h), dcol(C_KP + h), ALU.mult, ALU.add),
                     reads=[thf.k(), dv.k()], writes=[thf.k()], dur=pts_d(512))
                Pc = c2k.next()

                def scans(e, Pc=Pc, fg=fg):
                    r = None
                    for s in range(4):
                        r = e.tensor_tensor_scan(out=Pc[:, s * SUB:(s + 1) * SUB], data0=fg[:, s * SUB:(s + 1) * SUB],
                                                 data1=dv[:, C_ZERO:C_ZERO + 1].broadcast_to([128, SUB]), initial=1.0, op0=ALU.mult, op1=ALU.add)
                    return r
                P.op("DVE", scans, reads=[fg.k(), dv.k()], writes=[Pc.k()], dur=4 * dve_d(128))
                P.op("POOL", CP(dsv[:, h, :], Pc[:, SUB - 1::SUB]), reads=[Pc.k()], writes=[dsv.k(h)], dur=0.15)
                Pinv = fg
                P.op("ACT", A(Pinv[:, :], Pc[:, :], AF.Ln), reads=[Pc.k()], writes=[Pinv.k()],
                     dur=act_d(512), aset="lnexp")
                P.op("ACT", A(Pinv[:, :], Pinv[:, :], AF.Exp, scale=-1.0), reads=[Pinv.k()], writes=[Pinv.k()],
                     dur=act_d(512), aset="lnexp")
                bank2 = inproj_cm(8 + h)
                qs = c1k.next()
                P.op("ACT", A(qs[:, :], bank2[:, :], AF.Silu), reads=[bank2.k()], writes=[qs.k()],
                     dur=act_d(512), aset="silu")
                P.op("DVE", TTo(qeT[:, h, :], qs[:, :], Pc[:, :], ALU.mult),
                     reads=[qs.k(), Pc.k()], writes=[qeT.k(h)], dur=dve_d(512))
                P.op("POOL", TTo(keT[:, h, :], thf[:, :], Pinv[:, :], ALU.mult),
                     reads=[thf.k(), Pinv.k()], writes=[keT.k(h)], dur=ptt_d(512))
            qk_keys = [qeT.k(h) for h in range(4)] + [keT.k(h) for h in range(4)]
            ds_keys = [dsv.k(h) for h in range(4)]
            dsp_keys = [ds_prev.k(h) for h in range(4)]

            if t == 0:
                P.tag = "late_w"
                late_weights()
            P.tag = f"t{t}:GLA"
            for s in range(4):
                cs = slice(s * SUB, (s + 1) * SUB)
                v_bf = v_ring.next()
                gs_bf = gs_ring.next()
                for which in range(2):
                    c0 = 2048 + which * 512
                    bank = pb.next()
                    P.op("PE", MM([(bank[:, :], uT[:, kc, cs], w_in_bf[:, kc, c0:c0 + 512], kc == 0, kc == 7)
                                   for kc in range(8)]),
                         reads=[uT.k(s)] + [("w_in", 16 + which * 4 + i) for i in range(4)], writes=[bank.k()],
                         dur=8 * pe_d(512))
                    if which == 0:
                        P.op("ACT", A(v_bf[:, :], bank[:, :], AF.Copy), reads=[bank.k()], writes=[v_bf.k()], dur=act_d(512))
                    else:
                        P.op("ACT", A(gs_bf[:, :], bank[:, :], AF.Silu), reads=[bank.k()], writes=[gs_bf.k()],
                             dur=act_d(512), aset="silu")
                pk = pb.next()
                P.op("PE", TR([(pk.bf[:, h * 128:(h + 1) * 128], keT[:, h, cs]) for h in range(4)]),
                     reads=[keT.k(h) for h in range(4)] + [ident_bf.k()], writes=[pk.k()], dur=4 * pe_d(128))
                ket = ket_ring.next()
                P.op("ACT", A(ket[:, :], pk.bf[:, 0:512], AF.Copy), reads=[pk.k()], writes=[ket.k()], dur=act_d(512))
                psc = pb.next()
                P.op("PE", MM([(psc[:, h * 128:(h + 1) * 128], keT[:, h, cs], qeT[:, h, cs], True, True) for h in range(4)]),
                     reads=qk_keys, writes=[psc.k()], dur=4 * pe_d(128))
                scb = scb_ring.next()
                P.op("DVE", TTo(scb[:, :].rearrange("p (h c) -> p h c", h=4), psc[:, :].rearrange("p (h c) -> p h c", h=4),
                                mask_f[:, :].unsqueeze(1).broadcast_to([128, 4, 128]), ALU.mult),
                     reads=[psc.k(), mask_f.k()], writes=[scb.k()], dur=dve_d(512))
                pds = pb.next()
                P.op("PE", MM([(pds[:, h * 128:(h + 1) * 128], ket[:, h * 128:(h + 1) * 128], v_bf[:, h * 128:(h + 1) * 128], True, True)
                               for h in range(4)]),
                     reads=[ket.k(), v_bf.k()], writes=[pds.k()], dur=4 * pe_d(128))
                po = pb.next()
                items = []
                for h in range(4):
                    hb_ = slice(h * 128, (h + 1) * 128)
                    items.append((po[:, hb_], scb[:, hb_], v_bf[:, hb_], True, False))
                    items.append((po[:, hb_], qeT[:, h, cs], W_cur[:, h, :], False, True))
                P.op("PE", MM(items), reads=[scb.k(), v_bf.k(), W_cur.k()] + qk_keys, writes=[po.k()], dur=8 * pe_d(128))
                if s == 0:
                    dpb, dpi, dpk = ds_prev, 3, dsp_keys
                else:
                    dpb, dpi, dpk = dsv, s - 1, ds_keys
                for h in range(4):
                    P.op("DVE", STT(Z[:, h, :], Z[:, h, :], dpb[:, h, dpi:dpi + 1], pds[:, h * 128:(h + 1) * 128], ALU.mult, ALU.add),
                         reads=[Z.k(), pds.k()] + dpk, writes=[Z.k()], dur=dve_d(128))
                Wn = W_ring.next()
                for h in range(4):
                    P.op("POOL", TS(Wn[:, h, :], Z[:, h, :], dsv[:, h, s:s + 1], 0.0, ALU.mult, ALU.add),
                         reads=[Z.k()] + ds_keys, writes=[Wn.k()], dur=pts_d(128))
                W_cur = Wn
                st = st_gla.next()
                ybt = ybt_ring.next()
                for h in range(4):
                    P.op("ACT", A(ybt[:, h * 128:(h + 1) * 128], po[:, h * 128:(h + 1) * 128], AF.Square, accum=st[:, h:h + 1]),
                         reads=[po.k()], writes=[ybt.k(), st.k()], dur=act_d(128))
                rstd_ops(st, 0, 4, 8, 4, 1.0 / 128.0)
                for h in range(4):
                    hb_ = slice(h * 128, (h + 1) * 128)
                    P.op("DVE", STT(ybt[:, hb_], po[:, hb_], st[:, 8 + h:9 + h], gs_bf[:, hb_], ALU.mult, ALU.mult),
                         reads=[po.k(), st.k(), gs_bf.k()], writes=[ybt.k()], dur=dve_d(128))
                py = pb.next()
                P.op("PE", TR([(py.bf[:, h * 128:(h + 1) * 128], ybt[:, h * 128:(h + 1) * 128]) for h in range(4)]),
                     reads=[ybt.k(), ident_bf.k()], writes=[py.k()], dur=4 * pe_d(128))
                P.op("ACT", A(ybT[:, :, cs], py.bf[:, 0:512].rearrange("p (h c) -> p h c", h=4), AF.Copy),
                     reads=[py.k()], writes=[ybT.k(s)], dur=act_d(512))
            ds_prev = dsv

            P.tag = f"t{t}:SC"
            def sc1(s):
                cs = slice(s * SUB, (s + 1) * SUB)
                r0 = r_t + s * SUB
                xr = xr_ring.next()
                P.dma(xr[:, :], x_d[r0:r0 + SUB, :], writes=[xr.k()], nbytes=128 * D * 4)
                yk = [yaT.k(j) for j in range(4)] + [ybT.k(s)]
                for half in range(2):
                    hs_ = slice(half * 512, (half + 1) * 512)
                    ph = pc.next()
                    items = []
                    for cc in range(8):
                        lhsT = yaT[:, cc, cs] if cc < 4 else ybT[:, cc - 4, cs]
                        items.append((ph[:, :], lhsT, w_out_bf[:, cc, hs_], cc == 0, cc == 7))
                    P.op("PE", MM(items), reads=yk + [("w_out", kc) for kc in range(8)], writes=[ph.k()], dur=8 * pe_d(512))
                    P.op("DVE", TTo(xr[:, hs_], ph[:, :], xr[:, hs_], ALU.add), reads=[ph.k(), xr.k()], writes=[xr.k()],
                         dur=dve_d(512))
                hn = hn_ring.next()
                st = st_sc.next()
                P.op("ACT", A(hn[:, :], xr[:, :], AF.Square, accum=st[:, 0:1]), reads=[xr.k()], writes=[hn.k(), st.k()],
                     dur=act_d(1024))
                rstd_ops(st, 0, 1, 2, 1, 1.0 / D)
                P.op("POOL", TS(hn[:, :], xr[:, :], st[:, 2:3], 0.0, ALU.mult, ALU.add),
                     reads=[xr.k(), st.k()], writes=[hn.k()], dur=pts_d(1024))
                return xr, hn

            def sc2(s, xr, hn):
                cs = slice(s * SUB, (s + 1) * SUB)
                r0 = r_t + s * SUB
                pt = pc.next()
                P.op("PE", TR([(pt.bf[:, kc * 128:(kc + 1) * 128], hn[:, kc * 128:(kc + 1) * 128]) for kc in range(8)]),
                     reads=[hn.k(), ident_bf.k()], writes=[pt.k()], dur=8 * pe_d(128))
                hnT = hnT_ring.next()
                P.op("DVE", CP(hnT[:, :, :], pt.bf[:, :].rearrange("p (k n) -> p k n", k=8)),
                     reads=[pt.k()], writes=[hnT.k()], dur=dve_d(1024, 2))
                psub = p_ring.next()
                P.dma(psub[:, :], p_d[r0:r0 + SUB, :], writes=[psub.k()], nbytes=128 * PLE * 4)
                pbf = pbf_ring.next()
                P.op("POOL", CP(pbf[:, :], psub[:, :]), reads=[psub.k()], writes=[pbf.k()], dur=pts_d(256))
                ppt = pc.next()
                P.op("PE", TR([(ppt.bf[:, kc * 128:(kc + 1) * 128], pbf[:, kc * 128:(kc + 1) * 128]) for kc in range(2)]),
                     reads=[pbf.k(), ident_bf.k()], writes=[ppt.k()], dur=2 * pe_d(128))
                pT = pT_ring.next()
                P.op("ACT", A(pT[:, :, :], ppt.bf[:, 0:256].rearrange("p (k n) -> p k n", k=2), AF.Copy),
                     reads=[ppt.k()], writes=[pT.k()], dur=act_d(256))
                tg = tg_ring.next()
                for half in range(2):
                    hs_ = slice(half * 512, (half + 1) * 512)
                    pg = pc.next()
                    items = [(pg[:, :], hnT[:, kc, :], w_gate_bf[:, kc, hs_], kc == 0, False) for kc in range(8)]
                    items.append((pg[:, :], ones_bf[0:1, :], bg_bf[0:1, hs_], False, True))
                    P.op("PE", MM(items), reads=[hnT.k(), ones_bf.k(), bg_bf.k()] + [("w_gate", kc) for kc in range(8)],
                         writes=[pg.k()], dur=9 * pe_d(512))
                    P.op("ACT", A(tg[:, hs_], pg[:, :], AF.Tanh, scale=0.5), reads=[pg.k()], writes=[tg.k()],
                         dur=act_d(512), aset="silu")
                    ppp = pc.next()
                    P.op("PE", MM([(ppp[:, :], pT[:, kc, :], w_pp_bf[:, kc, hs_], kc == 0, kc == 1) for kc in range(2)]),
                         reads=[pT.k(), ("w_pp", 0), ("w_pp", 1)], writes=[ppp.k()], dur=2 * pe_d(512))
                    P.op("DVE", STT(tg[:, hs_], tg[:, hs_], 1.0, ppp[:, :], ALU.add, ALU.mult),
                         reads=[tg.k(), ppp.k()], writes=[tg.k()], dur=dve_d(512))
                P.op("DVE", STT(tg[:, :], tg[:, :], 0.5, xr[:, :], ALU.mult, ALU.add),
                     reads=[tg.k(), xr.k()], writes=[tg.k()], dur=dve_d(1024))
                st3 = st_sc.next()
                P.op("ACT", A(hnT[:, :, :].rearrange("p k n -> p (k n)"), tg[:, :], AF.Square, accum=st3[:, 0:1]), reads=[tg.k()], writes=[hnT.k(), st3.k()],
                     dur=act_d(1024))
                rstd_ops(st3, 0, 1, 2, 1, 1.0 / D)
                P.op("DVE", STT(tg[:, :], tg[:, :], st3[:, 2:3], fnw[:, :], ALU.mult, ALU.mult),
                     reads=[tg.k(), st3.k(), fnw.k()], writes=[tg.k()], dur=dve_d(1024))
                ok = ("out", r0)
                P.dma(out_d[r0:r0 + SUB, :], tg[:, :], reads=[tg.k()], writes=[ok], nbytes=128 * D * 4)
                out_keys.append(ok)

            pend = {}
            pend[0] = sc1(0)
            for s in range(1, 4):
                pend[s] = sc1(s)
                sc2(s - 1, *pend.pop(s - 1))
            sc2(3, *pend.pop(3))

        P.op("SP", lambda e: None, reads=out_keys, name="final")
        P.schedule()
        P.emit(block, sems, dsems)
        build_program.info = dict(n_switch=getattr(P, 'n_switch', 0), sbuf_bytes=sbytes[0], est_us=P.est_time, n_ops=len(P.ops), n_waits=P.n_waits)
    return nc


def host_layout(x, p, norm_mix_w, w_in, conv_w, conv_b, rg_wa, rg_ba, rg_wx, rg_bx,
                rg_lambda, hg_lb, hg_norm_w, w_out, ple_norm_w, w_ple_gate, b_ple_gate,
                w_ple_proj, final_norm_w):
    f = np.float32
    vecs = np.zeros((128, NV), f)
    vecs[:, V_NMW:V_NMW + 8] = np.asarray(norm_mix_w[0], f).reshape(8, 128).T
    vecs[:, V_PNW:V_PNW + 8] = np.asarray(ple_norm_w[0], f).reshape(8, 128).T
    cw = np.asarray(conv_w[0], f).reshape(4, 4, 128)
    for tap in range(4):
        vecs[:, V_CW + tap * 4:V_CW + tap * 4 + 4] = cw[tap].T
    vecs[:, V_CB:V_CB + 4] = np.asarray(conv_b[0], f).reshape(4, 128).T
    vecs[:, V_BA:V_BA + 4] = np.asarray(rg_ba[0], f).reshape(4, 128).T
    vecs[:, V_BX:V_BX + 4] = np.asarray(rg_bx[0], f).reshape(4, 128).T
    vecs[:, V_LAM:V_LAM + 4] = np.asarray(rg_lambda[0], f).reshape(4, 128).T
    vecs[:, V_LB0:V_LB0 + 4] = np.asarray(hg_lb[0], f).reshape(4, 128).T
    vecs[:, V_LB1:V_LB1 + 4] = np.asarray(hg_lb[1], f).reshape(4, 128).T
    vecs[:, V_NW] = np.asarray(hg_norm_w[0], f)

    def bd(w):
        w = np.asarray(w[0], f)
        o = np.zeros((128, 4, 128), f)
        for g in range(8):
            j, q = g // 2, g % 2
            o[q * 64:(q + 1) * 64, j, q * 64:(q + 1) * 64] = w[g]
        return np.ascontiguousarray(o.reshape(128, 512))
    shared = {
        "w_in": np.ascontiguousarray(np.asarray(w_in[0], f)),
        "w_out": np.ascontiguousarray(np.asarray(w_out[0], f)),
        "w_gate": np.ascontiguousarray(np.asarray(w_ple_gate[0], f)),
        "w_pp": np.ascontiguousarray(np.asarray(w_ple_proj[0], f)),
        "vecs": vecs,
        "fnw": np.ascontiguousarray(np.broadcast_to(np.asarray(final_norm_w, f)[None, :], (128, D))),
        "bg": np.ascontiguousarray(np.asarray(b_ple_gate[0], f)[None, :]),
        "wabd": bd(rg_wa),
        "wxbd": bd(rg_wx),
        "ident": np.eye(128, dtype=f),
        "mask": np.triu(np.ones((128, 128), f)),
    }
    return shared


_NC_CACHE = {}


def kernel(**inputs):
    x = np.asarray(inputs["x"], np.float32)
    p = np.asarray(inputs["p"], np.float32)
    B, T, _ = x.shape
    shared = host_layout(**inputs)
    if T not in _NC_CACHE:
        _NC_CACHE[T] = build_program(T)
    nc = _NC_CACHE[T]
    in_maps = []
    for b in range(B):
        m = dict(shared)
        m["x"] = np.ascontiguousarray(x[b])
        m["p"] = np.ascontiguousarray(p[0, b])
        in_maps.append(m)
    res = run_bass_kernel_spmd(nc, in_maps, core_ids=list(range(B)))
    return np.stack([np.asarray(r["out"], np.float32) for r in res.results], axis=0)
```
